# Optimizing a Trainium2 kernel written in Bass

```python
import math
import jax, jax.numpy as jnp
from jax import lax
import numpy as np

D_MODEL = 1024
BATCH = 4
SEQ = 4096
DEPTH = 2
DEC_BATCH = 128
DEC_SEQ = 8
PAST_LEN = 2048
PAGE_SIZE = 128

RET_HEADS = 4
RET_DK = 128
RET_DV = 128
RET_CHUNK = 128
RET_THETA = 10000.0
ATT_HEADS = 8
ATT_KV_HEADS = 2
ATT_HEAD_DIM = 64
ATT_GROUP = ATT_HEADS // ATT_KV_HEADS
ROPE_THETA = 500000.0
ROPE_DIM = ATT_HEAD_DIM // 4
IDX_HEADS = 4
IDX_DIM = 64
IDX_ROPE_DIM = IDX_DIM // 4
TOPK_MAX = 256
Q_BLOCK = 128
PLE_DIM = 256
NORM_EPS = 1e-6
GN_EPS = 1e-5

SPLIT_SIZES = (RET_HEADS * RET_DK, RET_HEADS * RET_DK, RET_HEADS * RET_DV, RET_HEADS * RET_DV,
               ATT_HEADS * ATT_HEAD_DIM, ATT_KV_HEADS * ATT_HEAD_DIM, ATT_KV_HEADS * ATT_HEAD_DIM,
               ATT_HEADS * ATT_HEAD_DIM, IDX_HEADS * IDX_DIM, IDX_DIM, IDX_HEADS, D_MODEL, D_MODEL)
IN_WIDTH = sum(SPLIT_SIZES)

kernel_name = "hybrid_retention_dsa_decode_step"


def rmsnorm(x, g):
    xf = x.astype(jnp.float32)
    r = xf * lax.rsqrt(jnp.mean(xf * xf, axis=-1, keepdims=True) + NORM_EPS)
    return (r * g.astype(jnp.float32)).astype(x.dtype)


def rope(x, pos, rot_dim, theta):
    half = rot_dim // 2
    freqs = jnp.exp(-math.log(theta) * jnp.arange(half, dtype=jnp.float32) / half)
    ang = pos.astype(jnp.float32)[:, None] * freqs[None, :]
    cos = jnp.cos(ang)[None, :, None, :]
    sin = jnp.sin(ang)[None, :, None, :]
    xf = x.astype(jnp.float32)
    x1 = xf[..., :half]
    x2 = xf[..., half:rot_dim]
    out = jnp.concatenate([x1 * cos - x2 * sin, x2 * cos + x1 * sin, xf[..., rot_dim:]], axis=-1)
    return out.astype(x.dtype)


def in_projection(h, w_in, pos, q_gain, k_gain):
    B, T, _ = h.shape
    z = jnp.einsum('btd,de->bte', h, w_in)
    offsets = np.cumsum(np.array(SPLIT_SIZES))[:-1].tolist()
    rq, rk, rv, rz, aq, ak, av, az, iq, ik, iw, gr, ga = jnp.split(z, offsets, axis=-1)
    rq = rope(rq.reshape(B, T, RET_HEADS, RET_DK), pos, RET_DK, RET_THETA)
    rk = rope(rk.reshape(B, T, RET_HEADS, RET_DK), pos, RET_DK, RET_THETA) * (RET_DK ** -0.5)
    rv = rv.reshape(B, T, RET_HEADS, RET_DV)
    aq = rope(rmsnorm(aq.reshape(B, T, ATT_HEADS, ATT_HEAD_DIM), q_gain), pos, ROPE_DIM, ROPE_THETA)
    ak = rope(rmsnorm(ak.reshape(B, T, ATT_KV_HEADS, ATT_HEAD_DIM), k_gain), pos, ROPE_DIM, ROPE_THETA)
    av = av.reshape(B, T, ATT_KV_HEADS, ATT_HEAD_DIM)
    iq = rope(iq.reshape(B, T, IDX_HEADS, IDX_DIM), pos, IDX_ROPE_DIM, ROPE_THETA)
    ik = rope(ik.reshape(B, T, 1, IDX_DIM), pos, IDX_ROPE_DIM, ROPE_THETA)[:, :, 0]
    return rq, rk, rv, rz, aq, ak, av, az, iq, ik, iw, gr, ga


def retention(q, k, v, s0):
    B, T, H, Dk = q.shape
    Dv = v.shape[-1]
    C = RET_CHUNK if T % RET_CHUNK == 0 else T
    n = T // C
    log_g = jnp.log1p(-jnp.exp2(-5.0 - jnp.arange(H, dtype=jnp.float32)))
    c = jnp.arange(C, dtype=jnp.float32)
    diff = c[:, None] - c[None, :]
    decay_intra = jnp.where(diff[None] >= 0, jnp.exp(jnp.maximum(diff, 0.0)[None] * log_g[:, None, None]), 0.0)
    decay_q = jnp.exp((c[:, None] + 1.0) * log_g[None, :])
    decay_k = jnp.exp((C - 1.0 - c)[:, None] * log_g[None, :])
    decay_s = jnp.exp(C * log_g)

    def to_chunks(a):
        return a.astype(jnp.float32).reshape(B, n, C, H, a.shape[-1]).transpose(1, 0, 2, 3, 4)

    def step(S, inp):
        qc, kc, vc = inp
        inner = jnp.einsum('bchd,bshd->bhcs', qc, kc) * decay_intra[None]
        o = (jnp.einsum('bhcs,bshe->bche', inner, vc)
             + jnp.einsum('bchd,bhde->bche', qc * decay_q[None, :, :, None], S))
        S = S * decay_s[None, :, None, None] + jnp.einsum('bchd,bche->bhde', kc * decay_k[None, :, :, None], vc)
        return S, o

    S, o = lax.scan(step, s0.astype(jnp.float32), (to_chunks(q), to_chunks(k), to_chunks(v)))
    o = o.transpose(1, 0, 2, 3, 4).reshape(B, T, H, Dv)
    return o, S


def sparse_attention(q, qpos, k_all, v_all, iq, iw, ik_all, topk):
    B, Tq = q.shape[:2]
    L = k_all.shape[1]
    causal = jnp.arange(L)[None, :] <= qpos[:, None]
    logits = jnp.einsum('bthd,bsd->bths', iq.astype(jnp.float32), ik_all.astype(jnp.float32)) * (IDX_DIM ** -0.5)
    score = jnp.einsum('bths,bth->bts', jax.nn.relu(logits), iw.astype(jnp.float32) * (IDX_HEADS ** -0.5))
    score = jnp.where(causal[None], score, -jnp.inf)
    _, sel = lax.top_k(score, topk)
    valid = sel <= qpos[None, :, None]
    kg = jax.vmap(lambda kb, ib: kb[ib])(k_all, sel)
    vg = jax.vmap(lambda vb, ib: vb[ib])(v_all, sel)
    qg = q.reshape(B, Tq, ATT_KV_HEADS, ATT_GROUP, ATT_HEAD_DIM).astype(jnp.float32)
    att = jnp.einsum('btkgd,btjkd->btkgj', qg, kg.astype(jnp.float32)) * (ATT_HEAD_DIM ** -0.5)
    att = jnp.where(valid[:, :, None, None, :], att, -jnp.inf)
    p = jax.nn.softmax(att, axis=-1)
    o = jnp.einsum('btkgj,btjkd->btkgd', p, vg.astype(jnp.float32))
    return o.reshape(B, Tq, ATT_HEADS * ATT_HEAD_DIM).astype(q.dtype)


def prompt_sparse_attention(aq, iq, iw, ak, av, ik, topk):
    B, T = aq.shape[:2]
    nb = T // Q_BLOCK

    def blk(a):
        return a.reshape(B, nb, Q_BLOCK, *a.shape[2:]).swapaxes(0, 1)

    pos_b = jnp.arange(T, dtype=jnp.int32).reshape(nb, Q_BLOCK)
    out = lax.map(lambda xs: sparse_attention(xs[0], xs[1], ak, av, xs[2], xs[3], ik, topk),
                  (blk(aq), pos_b, blk(iq), blk(iw)))
    return out.swapaxes(0, 1).reshape(B, T, -1)


def gather_pages(pool, page_table):
    g = pool[page_table]
    return g.reshape(page_table.shape[0], -1, *pool.shape[2:])


def group_norm_heads(o):
    mu = jnp.mean(o, axis=-1, keepdims=True)
    var = jnp.mean(jnp.square(o - mu), axis=-1, keepdims=True)
    return (o - mu) * lax.rsqrt(var + GN_EPS)


def decoder_layer(x, p_i, pos, ret_s0, past, topk, g_norm, w_in, q_gain, k_gain,
                  w_o_ret, w_o_att, w_out, w_ple_gate, w_ple_proj):
    B, T, _ = x.shape
    h = rmsnorm(x, g_norm)
    rq, rk, rv, rz, aq, ak, av, az, iq, ik, iw, gr, ga = in_projection(h, w_in, pos, q_gain, k_gain)
    o_r, s_new = retention(rq, rk, rv, ret_s0)
    o_r = group_norm_heads(o_r).reshape(B, T, RET_HEADS * RET_DV).astype(x.dtype)
    u_r = jnp.einsum('bte,ed->btd', o_r * jax.nn.silu(rz), w_o_ret)
    if past is None:
        o_a = prompt_sparse_attention(aq, iq, iw, ak, av, ik, topk)
    else:
        pk, pv, pik = past
        k_all = jnp.concatenate([pk.astype(ak.dtype), ak], axis=1)
        v_all = jnp.concatenate([pv.astype(av.dtype), av], axis=1)
        ik_all = jnp.concatenate([pik.astype(ik.dtype), ik], axis=1)
        o_a = sparse_attention(aq, pos, k_all, v_all, iq, iw, ik_all, topk)
    u_a = jnp.einsum('bte,ed->btd', o_a * jax.nn.silu(az), w_o_att)
    m = jax.nn.sigmoid(gr) * u_r + jax.nn.sigmoid(ga) * u_a
    x = x + jnp.einsum('btd,de->bte', m, w_out)
    gate = jax.nn.sigmoid(jnp.einsum('btd,de->bte', x, w_ple_gate))
    x = x + gate * jnp.einsum('btp,pd->btd', p_i, w_ple_proj)
    return x, ak, av, ik, s_new


def setup_inputs(seed: int = 0) -> dict:
    key = jax.random.key(seed)
    ks = jax.random.split(key, 20)
    n_pages = PAST_LEN // PAGE_SIZE
    n_used = DEC_BATCH * n_pages
    n_pool = n_used + n_used // 4
    f32 = jnp.float32
    nrm = lambda k, s, sc: jax.random.normal(k, s, f32) * sc
    x_prompt = nrm(ks[0], (BATCH, SEQ, D_MODEL), 1.0)
    x_sample = nrm(ks[1], (DEC_BATCH, DEC_SEQ, D_MODEL), 1.0)
    cache_k = nrm(ks[2], (DEPTH, n_pool, PAGE_SIZE, ATT_KV_HEADS, ATT_HEAD_DIM), 1.0)
    cache_v = nrm(ks[3], (DEPTH, n_pool, PAGE_SIZE, ATT_KV_HEADS, ATT_HEAD_DIM), 1.0)
    cache_idx_k = nrm(ks[4], (DEPTH, n_pool, PAGE_SIZE, IDX_DIM), 1.0)
    state_ret = nrm(ks[5], (DEPTH, DEC_BATCH, RET_HEADS, RET_DK, RET_DV), 0.5)
    perm = jax.random.permutation(ks[6], n_pool)[:n_used]
    page_table = perm.reshape(DEC_BATCH, n_pages).astype(jnp.int32)
    p_prompt = nrm(ks[7], (DEPTH, BATCH, SEQ, PLE_DIM), 1.0)
    p_sample = nrm(ks[8], (DEPTH, DEC_BATCH, DEC_SEQ, PLE_DIM), 1.0)
    norm_gain = 1.0 + nrm(ks[9], (DEPTH, D_MODEL), 0.1)
    w_in = nrm(ks[10], (DEPTH, D_MODEL, IN_WIDTH), D_MODEL ** -0.5)
    q_norm_gain = 1.0 + nrm(ks[11], (DEPTH, ATT_HEAD_DIM), 0.1)
    k_norm_gain = 1.0 + nrm(ks[12], (DEPTH, ATT_HEAD_DIM), 0.1)
    w_o_ret = nrm(ks[13], (DEPTH, RET_HEADS * RET_DV, D_MODEL), (RET_HEADS * RET_DV) ** -0.5)
    w_o_att = nrm(ks[14], (DEPTH, ATT_HEADS * ATT_HEAD_DIM, D_MODEL), (ATT_HEADS * ATT_HEAD_DIM) ** -0.5)
    w_out = nrm(ks[15], (DEPTH, D_MODEL, D_MODEL), D_MODEL ** -0.5)
    w_ple_gate = nrm(ks[16], (DEPTH, D_MODEL, D_MODEL), D_MODEL ** -0.5)
    w_ple_proj = nrm(ks[17], (DEPTH, PLE_DIM, D_MODEL), PLE_DIM ** -0.5)
    return {"x_prompt": x_prompt, "x_sample": x_sample, "cache_k": cache_k, "cache_v": cache_v,
            "cache_idx_k": cache_idx_k, "state_ret": state_ret, "page_table": page_table,
            "p_prompt": p_prompt, "p_sample": p_sample, "norm_gain": norm_gain, "w_in": w_in,
            "q_norm_gain": q_norm_gain, "k_norm_gain": k_norm_gain, "w_o_ret": w_o_ret,
            "w_o_att": w_o_att, "w_out": w_out, "w_ple_gate": w_ple_gate, "w_ple_proj": w_ple_proj}


def reference(x_prompt, x_sample, cache_k, cache_v, cache_idx_k, state_ret, page_table,
              p_prompt, p_sample, norm_gain, w_in, q_norm_gain, k_norm_gain, w_o_ret,
              w_o_att, w_out, w_ple_gate, w_ple_proj):
    Bp, Tp, _ = x_prompt.shape
    Bs, Ts, _ = x_sample.shape
    past_len = page_table.shape[1] * cache_k.shape[2]
    topk_prompt = min(TOPK_MAX, Tp // 4)
    topk_sample = min(TOPK_MAX, (past_len + Ts) // 4)
    pos_prompt = jnp.arange(Tp, dtype=jnp.int32)
    pos_sample = past_len + jnp.arange(Ts, dtype=jnp.int32)
    s0_prompt = jnp.zeros((Bp, RET_HEADS, RET_DK, RET_DV), jnp.float32)

    xp = x_prompt
    kp_l, vp_l, ikp_l, sp_l = [], [], [], []
    for i in range(DEPTH):
        xp, ak, av, ik, s_new = decoder_layer(
            xp, p_prompt[i], pos_prompt, s0_prompt, None, topk_prompt, norm_gain[i], w_in[i],
            q_norm_gain[i], k_norm_gain[i], w_o_ret[i], w_o_att[i], w_out[i], w_ple_gate[i], w_ple_proj[i])
        kp_l.append(ak); vp_l.append(av); ikp_l.append(ik); sp_l.append(s_new)

    xs = x_sample
    ks_l, vs_l, iks_l, ss_l = [], [], [], []
    for i in range(DEPTH):
        past = (gather_pages(cache_k[i], page_table), gather_pages(cache_v[i], page_table),
                gather_pages(cache_idx_k[i], page_table))
        xs, ak, av, ik, s_new = decoder_layer(
            xs, p_sample[i], pos_sample, state_ret[i], past, topk_sample, norm_gain[i], w_in[i],
            q_norm_gain[i], k_norm_gain[i], w_o_ret[i], w_o_att[i], w_out[i], w_ple_gate[i], w_ple_proj[i])
        ks_l.append(ak); vs_l.append(av); iks_l.append(ik); ss_l.append(s_new)

    k_prompt = jnp.stack(kp_l)
    v_prompt = jnp.stack(vp_l)
    idx_k_prompt = jnp.stack(ikp_l)
    ret_state_prompt = jnp.stack(sp_l)
    k_sample = jnp.stack(ks_l)
    v_sample = jnp.stack(vs_l)
    idx_k_sample = jnp.stack(iks_l)
    ret_state_sample = jnp.stack(ss_l)
    return (xp, xs, k_prompt, v_prompt, idx_k_prompt, ret_state_prompt,
            k_sample, v_sample, idx_k_sample, ret_state_sample)
```

```python
import math
import threading
import numpy as np
import ml_dtypes
import concourse.bass as bass
import concourse.mybir as mybir
from concourse.bass_utils import run_bass_kernel_spmd

F32 = mybir.dt.float32
BF = mybir.dt.bfloat16
I32 = mybir.dt.int32
AF = mybir.ActivationFunctionType
ALU = mybir.AluOpType
AX = mybir.AxisListType

D = 1024
INW = 5700
NCORES = 8
PLE = 256
BIG = 30000.0
NEG = -1.0e30
KBIS = 12
EPS_N = 1e-6
EPS_G = 1e-5
O_RQ, O_RK, O_RV, O_RZ, O_AQ, O_AK, O_AV, O_AZ, O_IQ, O_IK, O_IW, O_GR, O_GA = (
    0, 512, 1024, 1536, 2048, 2560, 2688, 2816, 3328, 3584, 3648, 3652, 4676)
NA = 3652
GDEC = [1.0 - 2.0 ** (-5.0 - h) for h in range(4)]

ENGS = ("pe", "act", "dve", "pool", "sp")


class Buf:
    __slots__ = ("name", "psum", "w", "rd", "sem", "cnt")

    def __init__(self, name, psum=False):
        self.name = name
        self.psum = psum
        self.w = None
        self.rd = []
        self.sem = None
        self.cnt = 0


class Tile:
    def __init__(self, h, bs):
        self.h = h
        self.bs = bs if isinstance(bs, list) else [bs]
        self.fresh = True

    @property
    def b(self):
        return self.bs[0]

    def __getitem__(self, k):
        return self.h[k]


def _bufs(ts):
    out = []
    for t in ts:
        if isinstance(t, Buf):
            out.append(t)
        else:
            out.extend(t.bs)
    return out


class Op:
    __slots__ = ("eng", "fn", "deps", "needed", "dma", "grp", "signal")

    def __init__(self, eng, fn, dma, grp):
        self.eng = eng
        self.fn = fn
        self.deps = []
        self.needed = False
        self.dma = dma
        self.grp = grp
        self.signal = None


_TL = threading.local()


class Baton:
    def __init__(self):
        self.active = False

    def run(self, fns, quotas):
        n = len(fns)
        if n == 1:
            fns[0]()
            return
        self.n = n
        self.turn = [threading.Semaphore(0) for _ in range(n)]
        self.done = [False] * n
        self.quota = [max(1, q) for q in quotas]
        self.cnt = 0
        self.exc = []
        self.active = True

        def wrap(k, f):
            self.turn[k].acquire()
            _TL.k = k
            try:
                f()
            except BaseException as e:
                self.exc.append(e)
            finally:
                self.done[k] = True
                self.cnt = 0
                nxt = self._next(k)
                if nxt is not None:
                    self.turn[nxt].release()

        th = [threading.Thread(target=wrap, args=(k, f)) for k, f in enumerate(fns)]
        for t_ in th:
            t_.start()
        self.turn[0].release()
        for t_ in th:
            t_.join()
        self.active = False
        _TL.k = None
        if self.exc:
            raise self.exc[0]

    def run_pair(self, fa, fb, qa=1, qb=1):
        self.run([fa, fb], [qa, qb])

    def _next(self, k):
        for d in range(1, self.n + 1):
            j = (k + d) % self.n
            if j != k and not self.done[j]:
                return j
        return None

    def tick(self):
        if not self.active:
            return
        k = _TL.k
        self.cnt += 1
        if self.cnt >= self.quota[k]:
            nxt = self._next(k)
            if nxt is not None:
                self.cnt = 0
                self.turn[nxt].release()
                self.turn[k].acquire()


class Sched:
    def __init__(self, nc):
        self.baton = Baton()
        self.nc = nc
        self.ops = {e: [] for e in ENGS}
        self.dmabufs = []
        self.nops = 0
        self.maxops = 10 ** 9

    def _dep(self, op, p):
        if p is None or p is op:
            return
        if p.eng == "pe" and op.eng == "pe" and p.dma is None and op.dma is None:
            return
        if op.grp is not None and p.grp == op.grp:
            return
        if p not in op.deps:
            op.deps.append(p)
        p.needed = True

    def add(self, eng, fn, reads=(), writes=(), dma=None, grp=None):
        if self.nops >= self.maxops:
            return None
        self.baton.tick()
        op = Op(eng, fn, dma, grp)
        if dma is not None and dma.sem is None:
            dma.sem = True
            self.dmabufs.append(dma)
        rb = _bufs(reads)
        wb = _bufs(writes)
        for b in rb:
            self._dep(op, b.w)
            if b.psum:
                for r in b.rd:
                    if r.eng != eng:
                        self._dep(op, r)
        for b in wb:
            self._dep(op, b.w)
            for r in b.rd:
                self._dep(op, r)
        for b in rb:
            b.rd.append(op)
        for b in wb:
            b.w = op
            b.rd = []
        self.ops[eng].append(op)
        self.nops += 1
        return op

    def emit(self):
        nc = self.nc
        engsem = {e: nc.alloc_semaphore("s_" + e) for e in ENGS}
        for b in self.dmabufs:
            b.sem = nc.alloc_semaphore("d_" + b.name)
        cnt = {e: 0 for e in ENGS}
        for e in ENGS:
            for op in self.ops[e]:
                if op.dma is not None:
                    op.dma.cnt += 16
                    op.signal = (op.dma.sem, op.dma.cnt, 16)
                elif op.needed:
                    cnt[e] += 1
                    op.signal = (engsem[e], cnt[e], 1)
        self.maxcnt = dict(cnt)
        names = {"pe": "tensor", "act": "scalar", "dve": "vector", "pool": "gpsimd", "sp": "sync"}
        with nc.Block() as block:
            for e in ENGS:
                ops = self.ops[e]

                def body(eng, ops=ops, e=e):
                    seen = {}
                    for op in ops:
                        for p in op.deps:
                            sem, val, _ = p.signal
                            k = id(sem)
                            if seen.get(k, 0) < val:
                                eng.wait_ge(sem, val)
                                seen[k] = val
                        inst = op.fn(eng)
                        if op.signal is not None:
                            inst.then_inc(op.signal[0], op.signal[2])
                    if e == "sp":
                        for b in self.dmabufs:
                            if b.cnt > 0:
                                eng.wait_ge(b.sem, b.cnt)
                        for e2 in ENGS:
                            if e2 != "sp" and cnt[e2] > 0:
                                eng.wait_ge(engsem[e2], cnt[e2])

                getattr(block, names[e])(body)


def bcast_mid(ap, n):
    s = ap.shape
    return ap.unsqueeze(1).to_broadcast([s[0], n, s[1]])


def bcast_last(ap, n):
    s = ap.shape
    return ap.unsqueeze(2).to_broadcast([s[0], s[1], n])


class Builder:
    def __init__(self, cfg):
        self.cfg = cfg
        self.T = cfg["T"]
        self.NB = self.T // 128
        self.DEPTH = cfg["DEPTH"]
        self.TOPK = min(256, self.T // 4)
        self.SAMPLE = cfg.get("SAMPLE", True)
        self.NPOOL = cfg.get("NPOOL", 2560)
        self.nc = bass.Bass("TRN2", target_bir_lowering=False)
        self.S = Sched(self.nc)
        self.S.maxops = cfg.get("MAXOPS", 10 ** 9)
        self.tiles = {}

    def sb(self, name, shape, dt):
        h = self.nc.alloc_sbuf_tensor(name, list(shape), dt)
        t = Tile(h, Buf(name))
        self.tiles[name] = t
        return t

    def view(self, name, ap, bufs=None):
        t = Tile(ap, bufs if bufs is not None else Buf(name))
        self.tiles[name] = t
        return t

    def ps(self, name, shape, dt):
        h = self.nc.alloc_psum_tensor(name, list(shape), dt)
        return Tile(h, Buf(name, psum=True))

    def dram(self, name, shape, dt, kind):
        return self.nc.dram_tensor(name, list(shape), dt, kind=kind).ap()

    def dma(self, eng, out_ap, in_ap, sb_tile, reads=(), writes=(), grp=None):
        return self.S.add(eng, lambda e: e.dma_start(out=out_ap, in_=in_ap),
                          reads=reads, writes=writes, dma=sb_tile.b, grp=grp)

    def gather(self, out_ap, in_ap, idx_ap, sb_tile, reads, writes):
        return self.S.add("pool", lambda e: e.indirect_dma_start(
            out=out_ap, out_offset=None, in_=in_ap,
            in_offset=bass.IndirectOffsetOnAxis(ap=idx_ap, axis=0)),
            reads=reads, writes=writes, dma=sb_tile.b)

    def mm(self, bank, out_ap, lhsT, rhs, reads, start=None, tp=None):
        if start is None:
            start = bank.fresh
        bank.fresh = False
        if tp is None:
            fn = lambda e: e.matmul(out_ap, lhsT, rhs, start=start, stop=True, skip_group_check=True)
        else:
            fn = lambda e: e.matmul(out_ap, lhsT, rhs, start=start, stop=True, skip_group_check=True,
                                    tile_position=tp)
        return self.S.add("pe", fn, reads=reads, writes=[bank])

    def tr(self, bank, out_ap, in_ap, ident_ap, reads):
        bank.fresh = False
        return self.S.add("pe", lambda e: e.transpose(out_ap, in_ap, ident_ap),
                          reads=reads, writes=[bank])

    def act(self, out_ap, in_ap, func, reads, writes, scale=1.0, bias=0.0, accum=None):
        if accum is None:
            fn = lambda e: e.activation(out=out_ap, in_=in_ap, func=func, bias=bias, scale=scale)
        else:
            fn = lambda e: e.activation(out=out_ap, in_=in_ap, func=func, bias=bias, scale=scale,
                                        accum_out=accum)
        return self.S.add("act", fn, reads=reads, writes=writes)

    def ts(self, eng, out_ap, in_ap, s1, op0, reads, writes, s2=None, op1=None, accum=None):
        kw = {}
        if op1 is not None:
            kw["op1"] = op1
        if accum is not None:
            kw["accum_out"] = accum
        return self.S.add(eng, lambda e: e.tensor_scalar(out=out_ap, in0=in_ap, scalar1=s1, scalar2=s2,
                                                         op0=op0, **kw),
                          reads=reads, writes=writes)

    def tt(self, eng, out_ap, a_ap, b_ap, op, reads, writes):
        return self.S.add(eng, lambda e: e.tensor_tensor(out=out_ap, in0=a_ap, in1=b_ap, op=op),
                          reads=reads, writes=writes)

    def stt(self, eng, out_ap, a_ap, scalar, b_ap, op0, op1, reads, writes):
        return self.S.add(eng, lambda e: e.scalar_tensor_tensor(out=out_ap, in0=a_ap, scalar=scalar,
                                                                in1=b_ap, op0=op0, op1=op1),
                          reads=reads, writes=writes)

    def red(self, eng, out_ap, in_ap, op, reads, writes):
        return self.S.add(eng, lambda e: e.tensor_reduce(out=out_ap, in_=in_ap, axis=AX.X, op=op),
                          reads=reads, writes=writes)

    def cp(self, eng, out_ap, in_ap, reads, writes):
        return self.S.add(eng, lambda e: e.tensor_copy(out=out_ap, in_=in_ap), reads=reads, writes=writes)

    def memset(self, eng, ap, val, writes):
        return self.S.add(eng, lambda e: e.memset(ap, val), writes=writes)

    def build(self):
        T, NB, DEPTH = self.T, self.NB, self.DEPTH
        TW = max(T, 4096)
        NROW = self.NPOOL * 8
        dr = self.dram
        self.xp = dr("xp", [T, D], F32, "ExternalInput")
        self.pp = dr("pp", [DEPTH, T, PLE], F32, "ExternalInput")
        self.w_in = dr("w_in", [DEPTH, D, INW], F32, "ExternalInput")
        self.w_or = dr("w_or", [DEPTH, 512, D], F32, "ExternalInput")
        self.w_oa = dr("w_oa", [DEPTH, 512, D], F32, "ExternalInput")
        self.w_out = dr("w_out", [DEPTH, D, D], F32, "ExternalInput")
        self.w_pg = dr("w_pg", [DEPTH, D, D], F32, "ExternalInput")
        self.w_pp = dr("w_pp", [DEPTH, PLE, D], F32, "ExternalInput")
        self.ng = dr("ng", [DEPTH, 128, 8], F32, "ExternalInput")
        self.qg = dr("qg", [DEPTH, 64], F32, "ExternalInput")
        self.kg = dr("kg", [DEPTH, 64], F32, "ExternalInput")
        self.c_ident = dr("c_ident", [128, 128], F32, "ExternalInput")
        self.c_bigi = dr("c_bigi", [128, 512], F32, "ExternalInput")
        self.c_rope = dr("c_rope", [NB + 1, 128, 288], F32, "ExternalInput")
        self.c_dtab = dr("c_dtab", [2, 128, 512], F32, "ExternalInput")
        self.c_misc = dr("c_misc", [2, 128, 64], F32, "ExternalInput")
        self.c_cbias = dr("c_cbias", [2, 128, 128], F32, "ExternalInput")
        self.c_rmask = dr("c_rmask", [128, 32], F32, "ExternalInput")
        self.c_sel = dr("c_sel", [128, 512], F32, "ExternalInput")
        self.yp = dr("yp", [T, D], F32, "ExternalOutput")
        self.kvi = dr("kvi", [DEPTH, T, 320], F32, "ExternalOutput")
        self.stp = dr("stp", [DEPTH, 128, 512], F32, "ExternalOutput")
        self.xbuf = dr("xbuf", [T + 128, D], F32, "Internal")
        self.gbuf = dr("gbuf", [T + 128, 1024], BF, "Internal")
        self.xbuf_b = [Buf("xbuf%d" % i) for i in range(NB + 1)]
        self.gbuf_b = [Buf("gbuf%d" % i) for i in range(NB + 1)]
        if self.SAMPLE:
            self.xs = dr("xs", [128, D], F32, "ExternalInput")
            self.pps = dr("pps", [DEPTH, 128, PLE], F32, "ExternalInput")
            self.st_in = dr("st_in", [DEPTH, 16, 4, 128, 128], F32, "ExternalInput")
            self.ck = [dr("ck%d" % l_, [NROW, 2048], F32, "ExternalInput") for l_ in range(DEPTH)]
            self.cv = [dr("cv%d" % l_, [NROW, 2048], F32, "ExternalInput") for l_ in range(DEPTH)]
            self.ci = [dr("ci%d" % l_, [NROW, 1024], F32, "ExternalInput") for l_ in range(DEPTH)]
            self.ptrep = dr("ptrep", [128, 16], I32, "ExternalInput")
            self.ys = dr("ys", [128, D], F32, "ExternalOutput")
            self.kvis = dr("kvis", [DEPTH, 128, 320], F32, "ExternalOutput")
            self.sts = dr("sts", [DEPTH, 16, 4, 128, 128], F32, "ExternalOutput")

        sb, ps, view = self.sb, self.ps, self.view
        self.WA = [sb("WA%d" % c, [128, NA], BF) for c in range(8)]
        self.Qreg = self.nc.alloc_sbuf_tensor("Qreg", [128, 16384], BF)
        q = self.Qreg
        self.scoresB = view("scoresB", q[:, 0:8192].bitcast(F32))
        self.ikTf = view("ikTf", q[0:64, 8192:12288])
        view("aqT2", q[:, 12288:12800])
        view("silu_az2", q[:, 12800:13312])
        view("G2", q[:, 13312:14336])
        self.WPP = sb("WPP", [128, 2, D], BF)
        self.scores = sb("scores", [128, TW], F32)
        ub = self.scores.h[:, 0:4096].bitcast(BF)
        self.WPG = view("WPG", ub.rearrange("p (c n) -> p c n", c=8), self.scores.bs)
        self.qpad = view("qpad", ub.rearrange("p (h s t) -> p h s t", h=4, s=16), self.scores.bs)
        self.ident = sb("ident", [128, 128], BF)
        self.bigi = sb("bigi", [128, 512], BF)
        self.dtab = sb("dtab", [128, 512], F32)
        self.misc = sb("misc", [128, 64], F32)
        self.cbias = sb("cbias", [128, 128], F32)
        self.gcol = sb("gcol", [128, 8], F32)
        self.qgb = sb("qgb", [128, 64], F32)
        self.kgb = sb("kgb", [128, 64], F32)
        self.rope = [sb("rope%d" % i, [128, 288], F32) for i in range(2)]
        self.ikT = sb("ikT", [128, TW // 2], BF)
        self.akT = sb("akT", [128, TW], BF)
        self.vaug = sb("vaug", [128, max(NB, 32), 2, 65], BF)
        self.S32 = sb("S32", [128, 512], F32)
        self.S16 = sb("S16", [128, 512], BF)
        self.xt = sb("xt", [128, D], F32)
        self.xn = sb("xn", [128, D], BF)
        self.hT = sb("hT", [128, 8, 128], BF)
        self.fA = sb("fA", [128, 520], F32)
        self.fB = sb("fB", [128, 520], F32)
        self.fC = sb("fC", [128, 520], F32)
        self.sm = sb("sm", [128, 64], F32)
        self.junk = sb("junk", [128, 4096], BF)
        jb = self.junk.bs
        self.thg = view("thg", self.junk.h[:, 0:2048], jb)
        self.mrg = view("mrg", self.junk.h[:, 2048:3072], jb)
        self.mT = view("mT", self.junk.h[:, 3072:4096].rearrange("p (c t) -> p c t", c=8), jb)
        for n in ["q_tm", "qd_tm", "qT", "qdT", "k_tm", "kT"]:
            sb(n, [128, 512], BF)
        t = self.tiles
        for n in ["R0", "R1", "PT0", "PT1", "m1a", "m1b", "aqT1", "silu_az1", "diag1"]:
            sb(n, [128, 512], BF)
        self.fD = sb("fD", [128, 520], F32)
        self.smB = sb("smB", [128, 16], F32)
        self.s1reg = self.nc.alloc_sbuf_tensor("s1reg", [128, 4096], BF)
        s1n = ["aq_bf", "aqT", "silu_az", "iq_bf", "v_tm", "v_dec", "innerD", "silu_rz"]
        for k, n in enumerate(s1n):
            view(n, self.s1reg[:, k * 512:(k + 1) * 512])
        view("GT", self.s1reg[:, 0:1024], t["aq_bf"].bs + t["aqT"].bs)
        self.thp = view("thp", self.s1reg[:, 1024:2048], t["silu_az"].bs + t["iq_bf"].bs)
        self.pbf = view("pbf", self.s1reg[:, 2048:2304], t["v_tm"].bs)
        self.pT = view("pT", self.s1reg[:, 2560:2816].rearrange("p (c t) -> p c t", c=2), t["v_dec"].bs)
        sb("diag0", [128, 512], BF)
        sb("G0", [128, 1024], BF)
        sb("G1", [128, 1024], BF)
        sb("iqT0", [128, 512], BF)
        sb("iqT1", [128, 512], BF)
        t["aqT0"] = t["aqT"]
        t["silu_az0"] = t["silu_az"]
        t["G"] = t["G0"]
        nblk = TW // 128
        self.ikTf_blk = [Tile(self.Qreg[0:64, 8192 + j * 128:8192 + (j + 1) * 128], Buf("ikTf_b%d" % j))
                         for j in range(32)]
        self.ikTf.bs = [x.b for x in self.ikTf_blk]
        qb_all = (self.scoresB.bs + self.ikTf.bs + t["aqT2"].bs + t["silu_az2"].bs + t["G2"].bs
                  + [Buf("Qrest")])
        self.WOR = Tile(q[:, 0:4096].rearrange("p (c n) -> p c n", c=4), qb_all)
        self.WOA = Tile(q[:, 4096:8192].rearrange("p (c n) -> p c n", c=4), qb_all)
        self.WOUT = Tile(q[:, 8192:16384].rearrange("p (c n) -> p c n", c=8), qb_all)
        self.akT_blk = [Tile(self.akT.h[:, j * 128:(j + 1) * 128], Buf("akT_b%d" % j)) for j in range(nblk)]
        self.ikT_blk = [Tile(self.ikT.h[(j % 2) * 64:(j % 2) * 64 + 64, (j // 2) * 128:(j // 2 + 1) * 128],
                             Buf("ikT_b%d" % j)) for j in range(nblk)]
        nvb = max(NB, 32)
        self.vaug_blk = [Tile(self.vaug.h[:, j, :, :], Buf("vaug_b%d" % j)) for j in range(nvb)]
        self.akT.bs = [x.b for x in self.akT_blk]
        self.ikT.bs = [x.b for x in self.ikT_blk]
        self.vaug.bs = [x.b for x in self.vaug_blk]
        self.kvi_t = sb("kvi_t", [128, 320], F32)
        self.akbf = sb("akbf", [128, 128], BF)
        self.ikbf = sb("ikbf", [128, 128], BF)
        self.bis = sb("bis", [128, 8 + KBIS], F32)
        self.bisB = sb("bisB", [128, 8 + KBIS], F32)
        self.scoresP = [self.scores, self.scoresB]
        self.bisP = [self.bis, self.bisB]
        self.pt = sb("pt", [128, PLE], F32)
        if self.SAMPLE:
            self.ones = sb("ones", [128, 128], BF)
            self.rmask = sb("rmask", [128, 32], F32)
            self.sel = sb("sel", [128, 512], BF)
            self.ptr_i = sb("ptr_i", [128, 16], I32)
            self.ptr_f = sb("ptr_f", [128, 16], F32)
            self.pidx = sb("pidx", [128, 16], I32)
            self.Kb = sb("Kb", [128, 2048], BF)
            self.dsel = view("dsel", self.Kb.h[:, :], self.Kb.bs)
            self.Vb = view("Vb", self.s1reg[:, 2048:4096],
                           t["v_tm"].bs + t["v_dec"].bs + t["innerD"].bs + t["silu_rz"].bs)
            self.Ib = sb("Ib", [128, 1024], BF)
            self.aknT = sb("aknT", [128, 128], BF)
            self.iknT = sb("iknT", [128, 128], BF)
            self.vnew = sb("vnew", [128, 2, 65], BF)
        print("sbuf bytes remaining", self.nc.sbuf_bytes_remaining)
        ab = self.akT.bs
        vf = self.vaug.h[:, 0:32, :, :].rearrange("p a b c -> p (a b c)")
        vb = self.vaug.bs
        ib = self.ikT.bs
        sbs = self.S32.bs
        T_ = Tile
        self.S2 = [
            dict(xt=self.xt, xn=self.xn, hT=self.hT, thg=self.thg, mrg=self.mrg, mT=self.mT, GT=t["GT"],
                 thp=self.thp, pbf=self.pbf, pT=self.pT, pt=self.pt, G=t["G0"], fA=self.fA, fB=self.fB,
                 sm=self.sm),
            dict(xt=T_(self.akT.h[:, 0:2048].bitcast(F32), ab[0:16]), thg=T_(self.akT.h[:, 2048:4096], ab[16:32]),
                 xn=T_(vf[:, 0:1024], vb), hT=T_(vf[:, 1024:2048].rearrange("p (c t) -> p c t", c=8), vb),
                 mrg=T_(vf[:, 2048:3072], vb), mT=T_(vf[:, 3072:4096].rearrange("p (c t) -> p c t", c=8), vb),
                 GT=T_(self.ikT.h[:, 0:1024], ib), thp=T_(self.ikT.h[:, 1024:2048], ib),
                 pt=T_(self.S32.h[:, 0:256], sbs), pbf=T_(self.S32.h[:, 256:384].bitcast(BF), sbs),
                 pT=T_(self.S32.h[:, 384:512].bitcast(BF).rearrange("p (c t) -> p c t", c=2), sbs),
                 G=t["G1"], fA=self.fD, fB=self.fC, sm=self.smB),
        ]
        self.PB = [ps("PB%d" % i, [128, 512], F32) for i in range(8)]
        self.PT = [Tile(self.PB[6 + i].h[:, :].bitcast(BF), self.PB[6 + i].bs) for i in range(2)]
        self._pbi = 0
        self._pti = 0
        self._pa = 0
        self._pb = 0
        self._s2mode = False
        self._s1mode = False

        self.load_consts()
        for l in range(DEPTH):
            self.load_layer_small(l)
            self.load_WA(l)
            self.load_tabs(0)
            if l > 0:
                self.memset("pool", self.vaug[:, :, :, 64:65], 1.0, [self.vaug])
            self.sweep1_A(l, 0)
            pipe = self.cfg.get("PIPE", True)
            for i in range(NB + 1):
                fns, qs = [], []
                if i + 1 < NB:
                    fns.append(lambda i=i: self.sweep1_A(l, i + 1)); qs.append(270)
                else:
                    fns.append(lambda: None); qs.append(1)
                if i < NB:
                    nch = (i + 4) // 4
                    fns.append(lambda i=i: self.indexer(l, i, i % 2)); qs.append(13 * nch + 50)
                else:
                    fns.append(lambda: None); qs.append(1)
                if i >= 1:
                    fns.append(lambda i=i: self.attention(l, i - 1, (i - 1) % 2)); qs.append(14 * i + 12)
                else:
                    fns.append(lambda: None); qs.append(1)
                if pipe:
                    nz = 14 * i + 12
                    self._q2 = [max(1, round(270 * 1.3 / (3 * KBIS + 6))), 1, max(1, round(nz * 0.6 / (3 * KBIS + 6)))]
                    self._s1mode = True
                    self.S.baton.run(fns, [3, 6, 1])
                    self._s1mode = False
                else:
                    for f_ in fns:
                        f_()
            self.store_state(l)
            self.load_W2(l)
            if self.SAMPLE:
                self.load_tabs(1)
                self.sweep1_block(l, NB, sample=True)
            self.load_gates(l)
            self.load_WPG(l)
            n2 = NB + (1 if self.SAMPLE else 0)
            self._s2mode = True
            self.sweep2_P(l, 0)
            for i in range(n2):
                if i + 1 < n2 and self.cfg.get("PIPE", True):
                    self.S.baton.run_pair(lambda i=i: self.sweep2_P(l, i + 1), lambda i=i: self.sweep2_Q(l, i), 1, 1)
                else:
                    if i + 1 < n2:
                        self.sweep2_P(l, i + 1)
                    self.sweep2_Q(l, i)
            self._s2mode = False
        self.S.emit()
        return self.nc

    def bank(self):
        k = getattr(_TL, "k", None) if self.S.baton.active else None
        if k is None:
            b = self.PB[self._pbi % 4]
            self._pbi += 1
        elif self._s1mode:
            if k == 0:
                b = self.PB[0]
            elif k == 1:
                b = self.PB[1]
            else:
                b = (self.PB[3], self.PB[7])[self._pb % 2]
                self._pb += 1
        elif k == 0:
            b = self.PB[self._pa % 2]
            self._pa += 1
        else:
            b = self.PB[2 + self._pb % 2]
            self._pb += 1
        b.fresh = True
        return b

    def tbank(self):
        k = getattr(_TL, "k", None) if self.S.baton.active else None
        if k is not None and self._s2mode:
            return self.PT[k]
        if k is not None and self._s1mode:
            return self.PT[0]
        b = self.PT[self._pti % 2]
        self._pti += 1
        return b

    def load_consts(self):
        self.dma("pool", self.ident[:, :], self.c_ident[:, :], self.ident, writes=[self.ident])
        self.dma("pool", self.bigi[:, :], self.c_bigi[:, :], self.bigi, writes=[self.bigi])
        self.memset("pool", self.vaug[:, :, :, :], 1.0, [self.vaug])
        self.memset("pool", self.ikbf[:, :], 0.0, [self.ikbf])
        if self.SAMPLE:
            self.dma("sp", self.rmask[:, :], self.c_rmask[:, :], self.rmask, writes=[self.rmask])
            self.dma("pool", self.sel[:, :], self.c_sel[:, :], self.sel, writes=[self.sel])
            self.dma("sp", self.ptr_i[:, :], self.ptrep[:, :], self.ptr_i, writes=[self.ptr_i])
            self.memset("pool", self.vnew[:, :, :], 1.0, [self.vnew])
            self.memset("pool", self.ones[:, :], 1.0, [self.ones])
            self.cp("pool", self.ptr_f[:, :], self.ptr_i[:, :], [self.ptr_i], [self.ptr_f])
            self.ts("pool", self.ptr_f[:, :], self.ptr_f[:, :], 8.0, ALU.mult, [self.ptr_f], [self.ptr_f])
            self.tt("pool", self.ptr_f[:, :], self.ptr_f[:, :], self.rmask[:, 16:17].to_broadcast([128, 16]),
                    ALU.add, [self.ptr_f, self.rmask], [self.ptr_f])
            self.cp("pool", self.pidx[:, :], self.ptr_f[:, :], [self.ptr_f], [self.pidx])

    def load_tabs(self, k):
        self.dma("sp", self.dtab[:, :], self.c_dtab[k], self.dtab, writes=[self.dtab])
        self.dma("sp", self.misc[:, :], self.c_misc[k], self.misc, writes=[self.misc])
        self.dma("sp", self.cbias[:, :], self.c_cbias[k], self.cbias, writes=[self.cbias])

    def load_layer_small(self, l):
        self.dma("sp", self.gcol[:, :], self.ng[l], self.gcol, writes=[self.gcol])
        self.dma("sp", self.qgb[:, :], self.qg[l].partition_broadcast(128), self.qgb, writes=[self.qgb])
        self.dma("sp", self.kgb[:, :], self.kg[l].partition_broadcast(128), self.kgb, writes=[self.kgb])

    def load_WA(self, l):
        for c in range(8):
            self.dma("pool", self.WA[c][:, :], self.w_in[l, c * 128:(c + 1) * 128, 0:NA], self.WA[c],
                     writes=[self.WA[c]])

    def load_gates(self, l):
        for c in range(8):
            self.dma("pool", self.WA[c][:, 0:2048], self.w_in[l, c * 128:(c + 1) * 128, O_GR:O_GR + 2048],
                     self.WA[c], writes=[self.WA[c]])

    def load_WPG(self, l):
        self.dma("pool", self.WPG[:, :, :], self.w_pg[l].rearrange("(c p) n -> p c n", p=128), self.WPG,
                 writes=[self.WPG])

    def load_W2(self, l):
        for wt, src in ((self.WOR, self.w_or), (self.WOA, self.w_oa), (self.WOUT, self.w_out),
                        (self.WPP, self.w_pp)):
            self.dma("pool", wt[:, :, :], src[l].rearrange("(c p) n -> p c n", p=128), wt, writes=[wt])

    def norm_hT(self, src_ap, src_bufs, ts=None):
        if ts is None:
            xt, xn, hT, sm = self.xt, self.xn, self.hT, self.sm
        else:
            xt, xn, hT, sm = ts["xt"], ts["xn"], ts["hT"], ts["sm"]
        self.dma("sp", xt[:, :], src_ap, xt, reads=src_bufs, writes=[xt])
        self.act(xn[:, :], xt[:, :], AF.Square, [xt], [xn, sm], accum=sm[:, 0:1])
        self.ts("pool", sm[:, 1:2], sm[:, 0:1], 1.0 / D, ALU.mult, [sm], [sm], s2=EPS_N, op1=ALU.add)
        self.tt("pool", sm[:, 2:3], sm[:, 1:2], self.misc[:, 12:13], ALU.pow, [sm, self.misc], [sm])
        self.act(xn[:, :], xt[:, :], AF.Copy, [xt, sm], [xn], scale=sm[:, 2:3])
        tb = self.tbank()
        for c in range(8):
            self.tr(tb, tb[:, c * 128:(c + 1) * 128], xn[:, c * 128:(c + 1) * 128], self.ident[:, :],
                    [xn, self.ident])
        self.tt("dve", hT[:, :, :], tb[:, :].rearrange("p (c t) -> p c t", c=8),
                bcast_last(self.gcol[:, :], 128), ALU.mult, [tb, self.gcol], [hT])

    def proj(self, bank, w_tiles, col0, ncols, hT=None):
        hT = self.hT if hT is None else hT
        for c in range(8):
            self.mm(bank, bank[:, 0:ncols], hT[:, c, :], w_tiles[c][:, col0:col0 + ncols],
                    [hT, w_tiles[c]])

    def rope_big(self, src, dst, tab):
        fB, fC = self.fB, self.fC
        cs = tab[:, 0:128]
        sn = tab[:, 128:256]
        s4 = src[:, 0:512].rearrange("p (h two d) -> p h two d", h=4, two=2)
        c4 = fC[:, 0:512].rearrange("p (h two d) -> p h two d", h=4, two=2)
        self.tt("pool", fB[:, 0:512].rearrange("p (h d) -> p h d", h=4),
                src[:, 0:512].rearrange("p (h d) -> p h d", h=4), bcast_mid(cs, 4), ALU.mult,
                [src, tab], [fB])
        self.tt("pool", c4[:, :, 0, :], s4[:, :, 1, :], bcast_mid(sn[:, 0:64], 4), ALU.mult, [src, tab], [fC])
        self.tt("pool", c4[:, :, 1, :], s4[:, :, 0, :], bcast_mid(sn[:, 64:128], 4), ALU.mult, [src, tab], [fC])
        self.tt("pool", dst[:, 0:512], fB[:, 0:512], fC[:, 0:512], ALU.add, [fB, fC], [dst])

    def headnorm_rope(self, fA, fB, nh, gain, tab):
        sm = self.sm
        w = nh * 64
        a3 = fA[:, 0:w].rearrange("p (h d) -> p h d", h=nh)
        self.red("dve", sm[:, 16:16 + nh], fB[:, 0:w].rearrange("p (h d) -> p h d", h=nh), ALU.add, [fB], [sm])
        self.ts("pool", sm[:, 24:24 + nh], sm[:, 16:16 + nh], 1.0 / 64, ALU.mult, [sm], [sm], s2=EPS_N, op1=ALU.add)
        self.tt("pool", sm[:, 32:32 + nh], sm[:, 24:24 + nh], self.misc[:, 12:13].to_broadcast([128, nh]), ALU.pow,
                [sm, self.misc], [sm])
        self.tt("pool", a3, a3, bcast_last(sm[:, 32:32 + nh], 64), ALU.mult, [fA, sm], [fA])
        self.tt("pool", a3, a3, bcast_mid(gain[:, :], nh), ALU.mult, [fA, gain], [fA])
        self.rope16(a3, nh, tab)

    def rope16(self, a3, nh, tab):
        fA, fC = self.fA, self.fC
        cs = tab[:, 256:272]
        sn = tab[:, 272:288]
        c3 = fC[:, 0:nh * 32].rearrange("p (h d) -> p h d", h=nh)
        self.tt("pool", c3[:, :, 0:16], a3[:, :, 0:16], bcast_mid(cs, nh), ALU.mult, [fA, tab], [fC])
        self.tt("pool", c3[:, :, 16:24], a3[:, :, 8:16], bcast_mid(sn[:, 0:8], nh), ALU.mult, [fA, tab], [fC])
        self.tt("pool", c3[:, :, 24:32], a3[:, :, 0:8], bcast_mid(sn[:, 8:16], nh), ALU.mult, [fA, tab], [fC])
        self.tt("pool", a3[:, :, 0:16], c3[:, :, 0:16], c3[:, :, 16:32], ALU.add, [fC], [fA])

    def sweep1_block(self, l, i, sample=False):
        self.sweep1_A(l, i, sample)
        if not sample:
            self.sweep1_B(l, i)

    def sweep1_B(self, l, i):
        self.indexer(l, i, i % 2)
        self.attention(l, i, i % 2)

    def sweep1_A(self, l, i, sample=False):
        t = self.tiles
        NB = self.NB
        par = i % 2
        par3 = (i % 2) if sample else (i % 3)
        ident = self.ident
        fA, fB, fC, sm, misc = self.fA, self.fB, self.fC, self.sm, self.misc
        tab = self.rope[i % 2]
        self.dma("sp", tab[:, :], self.c_rope[i], tab, writes=[tab])
        if sample:
            src = self.xs[:, :] if l == 0 else self.xbuf[NB * 128:(NB + 1) * 128, :]
        else:
            src = self.xp[i * 128:(i + 1) * 128, :] if l == 0 else self.xbuf[i * 128:(i + 1) * 128, :]
        self.norm_hT(src, [] if l == 0 else [self.xbuf_b[i]])
        WA = self.WA
        q_tm, qd_tm, qT, qdT = t["q_tm"], t["qd_tm"], t["qT"], t["qdT"]
        k_tm, kT, v_tm, v_dec = t["k_tm"], t["kT"], t["v_tm"], t["v_dec"]
        b = self.bank(); self.proj(b, WA, O_RQ, 512)
        self.act(fA[:, 0:512], b[:, :], AF.Copy, [b], [fA])
        self.rope_big(fA, q_tm, tab)
        self.tt("pool", qd_tm[:, :].rearrange("p (h d) -> p h d", h=4),
                q_tm[:, :].rearrange("p (h d) -> p h d", h=4), bcast_last(misc[:, 0:4], 128), ALU.mult,
                [q_tm, misc], [qd_tm])
        tb = self.tbank()
        for h in range(4):
            self.tr(tb, tb[:, h * 128:(h + 1) * 128], q_tm[:, h * 128:(h + 1) * 128], ident[:, :], [q_tm, ident])
        for h in range(4):
            self.tr(tb, tb[:, 512 + h * 128:512 + (h + 1) * 128], qd_tm[:, h * 128:(h + 1) * 128], ident[:, :],
                    [qd_tm, ident])
        self.cp("dve", qT[:, :], tb[:, 0:512], [tb], [qT])
        self.cp("dve", qdT[:, :], tb[:, 512:1024], [tb], [qdT])
        b = self.bank(); self.proj(b, WA, O_RK, 512)
        self.act(fA[:, 0:512], b[:, :], AF.Copy, [b], [fA], scale=128.0 ** -0.5)
        self.rope_big(fA, k_tm, tab)
        tb = self.tbank()
        for h in range(4):
            self.tr(tb, tb[:, h * 128:(h + 1) * 128], k_tm[:, h * 128:(h + 1) * 128], ident[:, :], [k_tm, ident])
        self.cp("dve", kT[:, :], tb[:, 0:512], [tb], [kT])
        b = self.bank(); self.proj(b, WA, O_RV, 512)
        self.act(v_tm[:, :], b[:, :], AF.Copy, [b], [v_tm])
        self.tt("pool", v_dec[:, :].rearrange("p (h d) -> p h d", h=4),
                v_tm[:, :].rearrange("p (h d) -> p h d", h=4), bcast_last(misc[:, 4:8], 128), ALU.mult,
                [v_tm, misc], [v_dec])
        b = self.bank(); self.proj(b, WA, O_RZ, 512)
        self.act(fA[:, 0:512], b[:, :], AF.Tanh, [b], [fA], scale=0.5)
        self.act(fB[:, 0:512], b[:, :], AF.Copy, [b], [fB], scale=0.5)
        self.tt("pool", fA[:, 0:512], fA[:, 0:512], fB[:, 0:512], ALU.mult, [fA, fB], [fA])
        self.tt("pool", t["silu_rz"][:, :], fA[:, 0:512], fB[:, 0:512], ALU.add, [fA, fB], [t["silu_rz"]])
        self.retention(l, i, sample, par3)
        b = self.bank(); self.proj(b, WA, O_AQ, 512)
        self.act(fA[:, 0:512], b[:, :], AF.Copy, [b], [fA])
        self.act(fB[:, 0:512], b[:, :], AF.Square, [b], [fB])
        self.headnorm_rope(fA, fB, 8, self.qgb, tab)
        aq_bf = t["aq_bf"]
        self.cp("pool", aq_bf[:, :].rearrange("p (h two d) -> p two h d", h=4, two=2),
                fA[:, 0:512].rearrange("p (two h d) -> p two h d", two=2, h=4), [fA], [aq_bf])
        tb = self.tbank()
        for h in range(4):
            self.tr(tb, tb[:, h * 128:(h + 1) * 128], aq_bf[:, h * 128:(h + 1) * 128], ident[:, :], [aq_bf, ident])
        self.cp("dve", t["aqT%d" % par3][:, :], tb[:, 0:512], [tb], [t["aqT%d" % par3]])
        kvi_t = self.kvi_t
        b = self.bank(); self.proj(b, WA, O_AK, 256)
        self.act(fA[:, 0:128], b[:, 0:128], AF.Copy, [b], [fA])
        self.act(fB[:, 0:128], b[:, 0:128], AF.Square, [b], [fB])
        self.act(kvi_t[:, 128:256], b[:, 128:256], AF.Copy, [b], [kvi_t])
        self.headnorm_rope(fA, fB, 2, self.kgb, tab)
        self.cp("pool", kvi_t[:, 0:128], fA[:, 0:128], [fA], [kvi_t])
        self.cp("pool", self.akbf[:, :], fA[:, 0:128], [fA], [self.akbf])
        vdst = self.vnew[:, :, 0:64] if sample else self.vaug_blk[i][:, :, 0:64]
        vt = self.vnew if sample else self.vaug_blk[i]
        self.cp("pool", vdst, kvi_t[:, 128:256].rearrange("p (g d) -> p g d", g=2), [kvi_t], [vt])
        tb = self.tbank()
        self.tr(tb, tb[:, 0:128], self.akbf[:, :], ident[:, :], [self.akbf, ident])
        if sample:
            self.cp("dve", self.aknT[:, :], tb[:, 0:128], [tb], [self.aknT])
        else:
            self.cp("dve", self.akT_blk[i][:, :], tb[:, 0:128], [tb], [self.akT_blk[i]])
        b = self.bank(); self.proj(b, WA, O_AZ, 512)
        self.act(fA[:, 0:512], b[:, :], AF.Tanh, [b], [fA], scale=0.5)
        self.act(fB[:, 0:512], b[:, :], AF.Copy, [b], [fB], scale=0.5)
        saz = t["silu_az%d" % par3]
        self.tt("pool", fA[:, 0:512], fA[:, 0:512], fB[:, 0:512], ALU.mult, [fA, fB], [fA])
        self.tt("pool", saz[:, :], fA[:, 0:512], fB[:, 0:512], ALU.add, [fA, fB], [saz])
        b = self.bank(); self.proj(b, WA, O_IQ, 324)
        self.act(fA[:, 0:324], b[:, 0:324], AF.Copy, [b], [fA])
        self.rope16(fA[:, 0:320].rearrange("p (h d) -> p h d", h=5), 5, tab)
        iq_bf = t["iq_bf"]
        i4 = iq_bf[:, :].rearrange("p (h two d) -> p h two d", h=4, two=2)
        f4 = fA[:, 0:256].rearrange("p (h d) -> p h d", h=4)
        self.cp("pool", i4[:, :, 0, :], f4, [fA], [iq_bf])
        self.cp("pool", i4[:, :, 1, :], f4, [fA], [iq_bf])
        self.cp("pool", kvi_t[:, 256:320], fA[:, 256:320], [fA], [kvi_t])
        half = 0
        self.cp("pool", self.ikbf[:, half * 64:(half + 1) * 64], fA[:, 256:320], [fA], [self.ikbf])
        diag = t["diag%d" % par]
        iqT = t["iqT%d" % par]
        self.ts("pool", sm[:, 8:12], fA[:, 320:324], 0.5, ALU.mult, [fA], [sm])
        self.tt("pool", diag[:, :].rearrange("p (h t) -> p h t", h=4), bcast_mid(ident[:, :], 4),
                bcast_last(sm[:, 8:12], 128), ALU.mult, [ident, sm], [diag])
        tb = self.tbank()
        for h in range(4):
            self.tr(tb, tb[:, h * 128:(h + 1) * 128], iq_bf[:, h * 128:(h + 1) * 128], ident[:, :], [iq_bf, ident])
        self.tr(tb, tb[:, 512:640], self.ikbf[:, :], ident[:, :], [self.ikbf, ident])
        self.cp("dve", iqT[:, :], tb[:, 0:512], [tb], [iqT])
        hs_ = slice(half * 64, (half + 1) * 64)
        if sample:
            self.cp("dve", self.iknT[0:64, :], tb[0:64, 512:640], [tb], [self.iknT])
            self.dma("sp", self.kvis[l], kvi_t[:, :], kvi_t, reads=[kvi_t])
            self.sample_attention(l, par, par3)
        else:
            self.cp("dve", self.ikTf_blk[i][:, :], tb[0:64, 512:640], [tb], [self.ikTf_blk[i]])
            self.dma("sp", self.kvi[l, i * 128:(i + 1) * 128, :], kvi_t[:, :], kvi_t, reads=[kvi_t])

    def retention(self, l, i, sample, par):
        t = self.tiles
        fA, fB, fC, sm, misc, ident = self.fA, self.fB, self.fC, self.sm, self.misc, self.ident
        qT, qdT, kT, k_tm, v_tm, v_dec = t["qT"], t["qdT"], t["kT"], t["k_tm"], t["v_tm"], t["v_dec"]
        innerD = t["innerD"]
        S32, S16 = self.S32, self.S16
        if i == 0 and not sample:
            self.memset("pool", S32[:, :], 0.0, [S32])
            self.memset("pool", S16[:, :], 0.0, [S16])
        b = self.bank()
        for h in range(4):
            hs = slice(h * 128, (h + 1) * 128)
            self.mm(b, b[:, hs], kT[:, hs], qT[:, hs], [kT, qT])
        self.tt("dve", innerD[:, :], b[:, :], self.dtab[:, :], ALU.mult, [b, self.dtab], [innerD])
        if sample:
            b = self.PB[5]
            b.fresh = True
        else:
            b = self.bank()
        if not sample:
            for h in range(4):
                hs = slice(h * 128, (h + 1) * 128)
                self.mm(b, b[:, hs], innerD[:, hs], v_tm[:, hs], [innerD, v_tm])
                self.mm(b, b[:, hs], qdT[:, hs], S16[:, hs], [qdT, S16])
        else:
            for h in range(4):
                hs = slice(h * 128, (h + 1) * 128)
                self.mm(b, b[:, hs], innerD[:, hs], v_tm[:, hs], [innerD, v_tm])
            qpad = self.qpad
            self.memset("pool", qpad[:, :, :, :], 0.0, [qpad])
            q4 = qdT[:, :].rearrange("p (h s q) -> p h s q", h=4, s=16)
            for s in range(16):
                self.cp("pool", qpad[:, :, s, s * 8:(s + 1) * 8], q4[:, :, s, :], [qdT], [qpad])
            for s in range(16):
                sl = slice((s % 2) * 512, (s % 2) * 512 + 512)
                self.dma("sp", self.xt[:, sl].rearrange("p (h e) -> p h e", h=4),
                         self.st_in[l, s].rearrange("h d e -> d h e"), self.xt, writes=[self.xt])
                self.cp("pool", S16[:, :], self.xt[:, sl], [self.xt], [S16])
                for h in range(4):
                    hs = slice(h * 128, (h + 1) * 128)
                    self.mm(b, b[:, hs], qpad[:, h, s, :], S16[:, hs], [qpad, S16])
                kp = t["kT"]
                self.tt("pool", kp[:, :], k_tm[:, :], self.rmask[:, s:s + 1].to_broadcast([128, 512]), ALU.mult,
                        [k_tm, self.rmask], [kp])
                kb = self.bank()
                for h in range(4):
                    hs = slice(h * 128, (h + 1) * 128)
                    self.mm(kb, kb[:, hs], kp[:, hs], v_dec[:, hs], [kp, v_dec])
                self.act(fB[:, 0:512], kb[:, :], AF.Copy, [kb], [fB])
                x4 = self.xt[:, sl].rearrange("p (h d) -> p h d", h=4)
                self.tt("pool", x4, x4, bcast_last(misc[:, 8:12], 128), ALU.mult, [self.xt, misc], [self.xt])
                self.tt("pool", self.xt[:, sl], self.xt[:, sl], fB[:, 0:512], ALU.add, [self.xt, fB], [self.xt])
                self.dma("sp", self.sts[l, s].rearrange("h d e -> d h e"),
                         self.xt[:, sl].rearrange("p (h e) -> p h e", h=4), self.xt, reads=[self.xt])
        self.act(fA[:, 0:512], b[:, :], AF.Copy, [b], [fA])
        self.act(fB[:, 0:512], b[:, :], AF.Square, [b], [fB])
        a3 = fA[:, 0:512].rearrange("p (h d) -> p h d", h=4)
        self.red("dve", sm[:, 40:44], a3, ALU.add, [fA], [sm])
        self.red("dve", sm[:, 44:48], fB[:, 0:512].rearrange("p (h d) -> p h d", h=4), ALU.add, [fB], [sm])
        self.ts("pool", sm[:, 40:44], sm[:, 40:44], 1.0 / 128, ALU.mult, [sm], [sm])
        self.tt("pool", sm[:, 48:52], sm[:, 40:44], sm[:, 40:44], ALU.mult, [sm], [sm])
        self.stt("dve", sm[:, 52:56], sm[:, 44:48], 1.0 / 128, sm[:, 48:52], ALU.mult, ALU.subtract,
                 [sm], [sm])
        self.ts("pool", sm[:, 52:56], sm[:, 52:56], EPS_G, ALU.add, [sm], [sm])
        self.tt("pool", sm[:, 52:56], sm[:, 52:56], misc[:, 12:13].to_broadcast([128, 4]), ALU.pow, [sm, misc], [sm])
        self.tt("pool", a3, a3, bcast_last(sm[:, 40:44], 128), ALU.subtract, [fA, sm], [fA])
        self.tt("pool", a3, a3, bcast_last(sm[:, 52:56], 128), ALU.mult, [fA, sm], [fA])
        G = t["G%d" % par]
        self.tt("pool", G[:, 0:512], fA[:, 0:512], t["silu_rz"][:, :], ALU.mult, [fA, t["silu_rz"]], [G])
        if sample:
            return
        b = self.bank()
        for h in range(4):
            hs = slice(h * 128, (h + 1) * 128)
            self.mm(b, b[:, hs], k_tm[:, hs], v_dec[:, hs], [k_tm, v_dec])
        self.act(fB[:, 0:512], b[:, :], AF.Copy, [b], [fB])
        s4 = S32[:, :].rearrange("p (h d) -> p h d", h=4)
        self.tt("pool", s4, s4, bcast_last(misc[:, 8:12], 128), ALU.mult, [S32, misc], [S32])
        self.tt("pool", S32[:, :], S32[:, :], fB[:, 0:512], ALU.add, [S32, fB], [S32])
        self.cp("pool", S16[:, :], S32[:, :], [S32], [S16])

    def store_state(self, l):
        self.dma("sp", self.stp[l], self.S32[:, :], self.S32, reads=[self.S32])

    def topk(self, nlo, n, TOPK, scores=None, bis=None):
        scores = self.scores if scores is None else scores
        bis = self.bis if bis is None else bis
        misc, junk = self.misc, self.junk
        thr = bis[:, 0:1]
        lo, hi, Rg, mid, cnt, sh = (bis[:, 1:2], bis[:, 2:3], bis[:, 3:4], bis[:, 4:5], bis[:, 5:6], bis[:, 6:7])
        H = bis[:, 8:8 + KBIS]
        self.red("dve", lo, scores[:, 0:nlo], ALU.min, [scores], [bis])
        self.red("dve", hi, scores[:, 0:n], ALU.max, [scores], [bis])
        self.tt("dve", Rg, hi, lo, ALU.subtract, [bis], [bis])
        self.ts("dve", H, misc[:, 16:16 + KBIS], Rg, ALU.mult, [misc, bis], [bis])
        self.stt("dve", mid, Rg, 0.5, lo, ALU.mult, ALU.add, [bis], [bis])
        for k in range(KBIS):
            self.ts("dve", junk[:, 0:n], scores[:, 0:n], mid, ALU.is_ge, [scores, bis], [junk, bis],
                    op1=ALU.add, accum=cnt)
            self.ts("dve", sh, cnt, TOPK - 0.5, ALU.is_ge, [bis], [bis], s2=0.5, op1=ALU.subtract)
            self.stt("dve", mid, sh, H[:, k:k + 1], mid, ALU.mult, ALU.add, [bis], [bis])
        self.stt("dve", thr, Rg, -(2.0 ** -(KBIS + 1)), mid, ALU.mult, ALU.add, [bis], [bis])

    def indexer(self, l, i, par, part=None):
        if part in (None, 0):
            self.indexer_scores(l, i, par)
        if part in (None, 1):
            if i * 128 < self.TOPK:
                self.memset("pool", self.bisP[par][:, 0:1], -1.0e29, [self.bisP[par]])
            else:
                bt = self.S.baton
                if bt.active and self._s1mode:
                    bt.quota[:] = self._q2
                self.topk(i * 128, (i + 1) * 128, self.TOPK, self.scoresP[par], self.bisP[par])

    def indexer_scores(self, l, i, par):
        t = self.tiles
        n = (i + 1) * 128
        scores = self.scoresP[par]
        diag, iqT = t["diag%d" % par], t["iqT%d" % par]
        R = [t["R0"], t["R1"]]
        nch = (n + 511) // 512
        threaded = self.S.baton.active and self._s1mode
        ri = 0
        for c in range(nch):
            c0 = c * 512
            w = min(512, n - c0)
            bs = self.PB[2] if threaded else self.PB[4]
            bs.fresh = True
            kt = Tile(self.Qreg[0:64, 8192 + c0:8192 + c0 + w],
                      [self.ikTf_blk[j].b for j in range(c * 4, c * 4 + w // 128)])
            for h in range(4):
                b = self.bank()
                self.mm(b, b[:, 0:w], iqT[0:64, h * 128:(h + 1) * 128], kt[:, :], [iqT, kt])
                r = R[ri % 2]; ri += 1
                self.act(r[:, 0:w], b[:, 0:w], AF.Relu, [b], [r], scale=0.125)
                self.mm(bs, bs[:, 0:w], diag[:, h * 128:(h + 1) * 128], r[:, 0:w], [diag, r])
            self.act(scores[:, c0:c0 + w], bs[:, 0:w], AF.Copy, [bs], [scores])
        self.tt("pool", scores[:, i * 128:n], scores[:, i * 128:n], self.cbias[:, :], ALU.add,
                [scores, self.cbias], [scores])

    def attention(self, l, i, par):
        t = self.tiles
        n = (i + 1) * 128
        scores, bis = self.scoresP[par], self.bisP[par]
        par3 = i % 3
        aqT = t["aqT%d" % par3]
        m1 = [t["m1a"], t["m1b"]]
        PTs = [t["PT0"], t["PT1"]]
        thr = bis[:, 0:1]
        oacc = [self.PB[4], self.PB[5]]
        oacc[0].fresh = True
        oacc[1].fresh = True
        pi = 0
        nch = (n + 511) // 512
        for c in range(nch):
            c0 = c * 512
            w = min(512, n - c0)
            mk = m1[c % 2]
            self.ts("dve", mk[:, 0:w], scores[:, c0:c0 + w], thr, ALU.is_ge, [scores, bis], [mk],
                    s2=1.0, op1=ALU.subtract)
            for jj in range(w // 128):
                j = c * 4 + jj
                bb = [self.bank(), self.bank()]
                for g in range(2):
                    ps_ = slice(g * 64, (g + 1) * 64)
                    self.mm(bb[g], bb[g][:, :], self.akT_blk[j][ps_, :], aqT[ps_, :], [self.akT_blk[j], aqT])
                for g in range(2):
                    self.mm(bb[g], bb[g][:, :], mk[:, jj * 128:(jj + 1) * 128], self.bigi[:, :], [mk, self.bigi])
                for g in range(2):
                    pt = PTs[g]
                    self.act(pt[:, :], bb[g][:, :], AF.Exp, [bb[g]], [pt], scale=0.125)
                    for h in range(4):
                        self.mm(oacc[g], oacc[g][:, h * 65:(h + 1) * 65], pt[:, h * 128:(h + 1) * 128],
                                self.vaug_blk[j][:, g, :], [pt, self.vaug_blk[j]])
        fD = self.fD
        for g in range(2):
            self.act(fD[:, g * 260:(g + 1) * 260], oacc[g][:, 0:260], AF.Copy, [oacc[g]], [fD])
        self.finish_attention(i, par3, fD)

    def finish_attention(self, i, par, src):
        t = self.tiles
        sm = self.smB
        G = t["G%d" % par]
        saz = t["silu_az%d" % par]
        a3 = src[:, 0:520].rearrange("p (h d) -> p h d", h=8)
        self.tt("pool", sm[:, 0:8].unsqueeze(2), a3[:, :, 64:65],
                self.misc[:, 13:14].unsqueeze(1).to_broadcast([128, 8, 1]), ALU.pow, [src, self.misc], [sm])
        self.tt("pool", a3[:, :, 0:64], a3[:, :, 0:64], bcast_last(sm[:, 0:8], 64), ALU.mult, [src, sm], [src])
        self.tt("pool", G[:, 512:1024].rearrange("p (h d) -> p h d", h=8), a3[:, :, 0:64],
                saz[:, :].rearrange("p (h d) -> p h d", h=8), ALU.mult, [src, saz], [G])
        self.dma("sp", self.gbuf[i * 128:(i + 1) * 128, :], G[:, :], G, reads=[G], writes=[self.gbuf_b[i]])

    @staticmethod
    def col0(r):
        return ((r % 2) * 2 + (r // 2) // 4) * 512 + ((r // 2) % 4) * 128

    def sample_attention(self, l, par, par3):
        t = self.tiles
        NB = self.NB
        ident, bigi, sm = self.ident, self.bigi, self.sm
        scores, bis, fA, fB = self.scores, self.bis, self.fA, self.fB
        iqT, diag, aqT = t["iqT%d" % par], t["diag%d" % par], t["aqT%d" % par3]
        R = [t["R0"], t["R1"]]
        PTs = [t["PT0"], t["PT1"]]
        akT, ikT, vaug = self.akT, self.ikT, self.vaug
        Kb, Vb, Ib, dsel = self.Kb, self.Vb, self.Ib, self.dsel
        wb = self.bank()
        self.mm(wb, wb[:, :], self.ones[:, :], diag[:, :], [diag, self.ones])
        for g in range(4):
            self.tt("dve", dsel[:, g * 512:(g + 1) * 512].rearrange("p (h t) -> p h t", h=4), wb[:, :].rearrange(
                "p (h t) -> p h t", h=4), bcast_mid(self.sel[:, g * 128:(g + 1) * 128], 4), ALU.mult,
                [wb, self.sel], [dsel])
        ri = 0
        for g in range(4):
            for k in range(4):
                s = 4 * g + k
                self.gather(Ib[:, :], self.ci[l][:, :], self.pidx[:, s:s + 1], Ib, [self.pidx], [Ib])
                tb = self.tbank()
                for rr in range(8):
                    self.tr(tb, tb[:, rr * 128:(rr + 1) * 128], Ib[:, rr * 128:(rr + 1) * 128], ident[:, :],
                            [Ib, ident])
                self.cp("dve", akT[:, k * 1024:(k + 1) * 1024], tb[:, :], [tb], [akT])
            for cc in range(4):
                half, cq = cc // 2, cc % 2
                hf = slice(half * 64, half * 64 + 64)
                bs = self.PB[4]
                bs.fresh = True
                for h in range(4):
                    b = self.bank()
                    for k in range(4):
                        s = 4 * g + k
                        self.mm(b, b[32 * k:32 * k + 8, :], iqT[hf, h * 128 + 8 * s:h * 128 + 8 * s + 8],
                                akT[hf, k * 1024 + cq * 512:k * 1024 + cq * 512 + 512], [iqT, akT],
                                start=True, tp=(half * 64, 32 * k))
                    r = R[ri % 2]; ri += 1
                    self.act(r[:, :], b[:, :], AF.Relu, [b], [r], scale=0.125)
                    self.mm(bs, bs[:, :], dsel[:, g * 512 + h * 128:g * 512 + (h + 1) * 128], r[:, :], [dsel, r])
                if g == 0:
                    self.act(scores[:, cc * 512:(cc + 1) * 512], bs[:, :], AF.Copy, [bs], [scores])
                else:
                    self.act(fB[:, 0:512], bs[:, :], AF.Copy, [bs], [fB])
                    self.tt("pool", scores[:, cc * 512:(cc + 1) * 512], scores[:, cc * 512:(cc + 1) * 512],
                            fB[:, 0:512], ALU.add, [scores, fB], [scores])
        bs = self.PB[4]
        bs.fresh = True
        for h in range(4):
            b = self.bank()
            self.mm(b, b[:, 0:128], iqT[0:64, h * 128:(h + 1) * 128], self.iknT[0:64, :], [iqT, self.iknT])
            r = R[ri % 2]; ri += 1
            self.act(r[:, 0:128], b[:, 0:128], AF.Relu, [b], [r], scale=0.125)
            self.mm(bs, bs[:, 0:128], diag[:, h * 128:(h + 1) * 128], r[:, 0:128], [diag, r])
        self.act(scores[:, 2048:2176], bs[:, 0:128], AF.Copy, [bs], [scores])
        self.tt("pool", scores[:, 2048:2176], scores[:, 2048:2176], self.cbias[:, :], ALU.add,
                [scores, self.cbias], [scores])
        self.topk(2048, 2176, 256)
        m1s = self.junk
        self.ts("dve", m1s[:, 0:2176], scores[:, 0:2176], bis[:, 0:1], ALU.is_ge, [scores, bis], [m1s],
                s2=1.0, op1=ALU.subtract)
        onew = [self.PB[4], self.PB[5]]
        pi = 0
        for g in range(2):
            onew[g].fresh = True
            ps_ = slice(g * 64, (g + 1) * 64)
            b = self.bank()
            self.mm(b, b[:, :], self.aknT[ps_, :], aqT[ps_, :], [self.aknT, aqT])
            self.mm(b, b[:, :], m1s[:, 2048:2176], bigi[:, :], [m1s, bigi])
            pt = PTs[pi % 2]; pi += 1
            self.act(pt[:, :], b[:, :], AF.Exp, [b], [pt], scale=0.125)
            for h in range(4):
                self.mm(onew[g], onew[g][:, h * 65:(h + 1) * 65], pt[:, h * 128:(h + 1) * 128],
                        self.vnew[:, g, :], [pt, self.vnew])
        for g in range(2):
            self.act(fA[:, g * 260:(g + 1) * 260], onew[g][:, 0:260], AF.Copy, [onew[g]], [fA])
        oT = [self.PB[4], self.PB[5]]
        oT[0].fresh = True
        oT[1].fresh = True
        aq4 = aqT[:, :].rearrange("p (h t) -> p h t", h=4)
        bg4 = bigi[:, :].rearrange("p (h t) -> p h t", h=4)
        for s in range(16):
            slot = s % 2
            self.gather(Kb[:, :], self.ck[l][:, :], self.pidx[:, s:s + 1], Kb, [self.pidx], [Kb])
            self.gather(Vb[:, :], self.cv[l][:, :], self.pidx[:, s:s + 1], Vb, [self.pidx], [Vb])
            for hh in range(2):
                tb = self.tbank()
                for r8 in range(8):
                    r = hh * 8 + r8
                    self.tr(tb, tb[:, r8 * 128:(r8 + 1) * 128], Kb[:, r * 128:(r + 1) * 128], ident[:, :],
                            [Kb, ident])
                self.cp("dve", akT[:, slot * 2048 + hh * 1024:slot * 2048 + (hh + 1) * 1024], tb[:, :], [tb], [akT])
            self.cp("pool", vaug[:, slot * 16:(slot + 1) * 16, :, 0:64],
                    Vb[:, :].rearrange("p (r g d) -> p r g d", r=16, g=2), [Vb], [vaug])
            for g in range(2):
                ps_ = slice(g * 64, (g + 1) * 64)
                b = self.bank()
                for r in range(16):
                    reg = b[:, r * 32:(r + 1) * 32].rearrange("p (h q) -> p h q", h=4)
                    self.mm(b, reg, akT[ps_, slot * 2048 + r * 128:slot * 2048 + (r + 1) * 128],
                            aq4[ps_, :, 8 * s:8 * s + 8], [akT, aqT])
                    c0 = self.col0(r)
                    self.mm(b, reg, m1s[:, c0:c0 + 128], bg4[:, :, 8 * s:8 * s + 8], [m1s, bigi])
                pt = PTs[pi % 2]; pi += 1
                self.act(pt[:, :], b[:, :], AF.Exp, [b], [pt], scale=0.125)
                oreg = oT[g][0:65, :].rearrange("p (h s q) -> p h s q", h=4, s=16)[:, :, s, :]
                for r in range(16):
                    self.mm(oT[g], oreg, vaug[:, slot * 16 + r, g, :],
                            pt[:, r * 32:(r + 1) * 32].rearrange("p (h q) -> p h q", h=4), [vaug, pt])
        oTb = Kb
        for g in range(2):
            self.act(oTb[0:65, g * 512:(g + 1) * 512], oT[g][0:65, :], AF.Copy, [oT[g]], [oTb])
        tb = self.tbank()
        for g in range(2):
            for h in range(4):
                hh = g * 4 + h
                self.tr(tb, tb[:, hh * 66:hh * 66 + 65], oTb[0:65, g * 512 + h * 128:g * 512 + (h + 1) * 128],
                        ident[0:65, 0:65], [oTb, ident])
        self.cp("dve", fB[:, 0:520].rearrange("p (h d) -> p h d", h=8),
                tb[:, 0:528].rearrange("p (h d) -> p h d", h=8)[:, :, 0:65], [tb], [fB])
        self.tt("pool", fA[:, 0:520], fA[:, 0:520], fB[:, 0:520], ALU.add, [fA, fB], [fA])
        self.finish_attention(NB, par3, fA)

    def sweep2_P(self, l, i):
        NB = self.NB
        sample = (i == NB)
        ident = self.ident
        S = self.S2[i % 2]
        xt, fA, fB = S["xt"], S["fA"], S["fB"]
        G, GT, thg, mrg = S["G"], S["GT"], S["thg"], S["mrg"]
        if sample:
            src = self.xs[:, :] if l == 0 else self.xbuf[NB * 128:(NB + 1) * 128, :]
            psrc = self.pps[l]
        else:
            src = self.xp[i * 128:(i + 1) * 128, :] if l == 0 else self.xbuf[i * 128:(i + 1) * 128, :]
            psrc = self.pp[l, i * 128:(i + 1) * 128, :]
        self.dma("sp", S["pt"][:, :], psrc, S["pt"], writes=[S["pt"]])
        self.dma("sp", G[:, :], self.gbuf[i * 128:(i + 1) * 128, :], G, reads=[self.gbuf_b[i]], writes=[G])
        tb = self.tbank()
        for c in range(8):
            self.tr(tb, tb[:, c * 128:(c + 1) * 128], G[:, c * 128:(c + 1) * 128], ident[:, :], [G, ident])
        self.cp("dve", GT[:, :], tb[:, :], [tb], [GT])
        self.norm_hT(src, [] if l == 0 else [self.xbuf_b[i]], S)
        WA = self.WA
        for q in range(4):
            b = self.bank(); self.proj(b, WA, q * 512, 512, S["hT"])
            self.act(thg[:, q * 512:(q + 1) * 512], b[:, :], AF.Tanh, [b], [thg], scale=0.5)
        for hf in range(2):
            cs = slice(hf * 512, (hf + 1) * 512)
            bu = self.bank()
            for c in range(4):
                self.mm(bu, bu[:, :], GT[:, c * 128:(c + 1) * 128], self.WOR[:, c, cs], [GT, self.WOR])
            self.stt("dve", fA[:, 0:512], thg[:, hf * 512:(hf + 1) * 512], 1.0, bu[:, :], ALU.add, ALU.mult,
                     [thg, bu], [fA])
            bu = self.bank()
            for c in range(4):
                self.mm(bu, bu[:, :], GT[:, 512 + c * 128:512 + (c + 1) * 128], self.WOA[:, c, cs], [GT, self.WOA])
            self.stt("dve", fB[:, 0:512], thg[:, 1024 + hf * 512:1024 + (hf + 1) * 512], 1.0, bu[:, :],
                     ALU.add, ALU.mult, [thg, bu], [fB])
            self.tt("pool", mrg[:, cs], fA[:, 0:512], fB[:, 0:512], ALU.add, [fA, fB], [mrg])

    def sweep2_Q(self, l, i):
        NB = self.NB
        sample = (i == NB)
        last = (l == self.DEPTH - 1)
        ident = self.ident
        S = self.S2[i % 2]
        xt, fA, xn = S["xt"], S["fA"], S["xn"]
        mrg, mT, thp, pbf, pT, pt = S["mrg"], S["mT"], S["thp"], S["pbf"], S["pT"], S["pt"]
        tb = self.tbank()
        for c in range(8):
            self.tr(tb, tb[:, c * 128:(c + 1) * 128], mrg[:, c * 128:(c + 1) * 128], ident[:, :], [mrg, ident])
        self.cp("dve", mT[:, :, :], tb[:, :].rearrange("p (c t) -> p c t", c=8), [tb], [mT])
        for hf in range(2):
            cs = slice(hf * 512, (hf + 1) * 512)
            bu = self.bank()
            for c in range(8):
                self.mm(bu, bu[:, :], mT[:, c, :], self.WOUT[:, c, cs], [mT, self.WOUT])
            self.stt("dve", xt[:, cs], bu[:, :], 0.5, xt[:, cs], ALU.mult, ALU.add, [bu, xt], [xt])
        self.cp("pool", xn[:, :], xt[:, :], [xt], [xn])
        tb = self.tbank()
        for c in range(8):
            self.tr(tb, tb[:, c * 128:(c + 1) * 128], xn[:, c * 128:(c + 1) * 128], ident[:, :], [xn, ident])
        self.cp("dve", mT[:, :, :], tb[:, :].rearrange("p (c t) -> p c t", c=8), [tb], [mT])
        self.cp("pool", pbf[:, :], pt[:, :], [pt], [pbf])
        tb = self.tbank()
        for c in range(2):
            self.tr(tb, tb[:, c * 128:(c + 1) * 128], pbf[:, c * 128:(c + 1) * 128], ident[:, :], [pbf, ident])
        self.cp("dve", pT[:, :, :], tb[:, 0:256].rearrange("p (c t) -> p c t", c=2), [tb], [pT])
        for hf in range(2):
            cs = slice(hf * 512, (hf + 1) * 512)
            bu = self.bank()
            for c in range(8):
                self.mm(bu, bu[:, :], mT[:, c, :], self.WPG[:, c, cs], [mT, self.WPG])
            self.act(thp[:, cs], bu[:, :], AF.Tanh, [bu], [thp], scale=0.5)
            bu = self.bank()
            for c in range(2):
                self.mm(bu, bu[:, :], pT[:, c, :], self.WPP[:, c, cs], [pT, self.WPP])
            self.stt("dve", fA[:, 0:512], thp[:, cs], 1.0, bu[:, :], ALU.add, ALU.mult, [thp, bu], [fA])
            self.stt("dve", xt[:, cs], fA[:, 0:512], 0.5, xt[:, cs], ALU.mult, ALU.add, [fA, xt], [xt])
        if last:
            dst = self.ys[:, :] if sample else self.yp[i * 128:(i + 1) * 128, :]
        else:
            dst = self.xbuf[i * 128:(i + 1) * 128, :]
        self.dma("sp", dst, xt[:, :], xt, reads=[xt], writes=[] if last else [self.xbuf_b[i]])


def host_consts(T):
    NB = T // 128
    ident = np.eye(128, dtype=np.float32)
    bigi = np.tile(ident * BIG, (1, 4)).astype(np.float32)
    pos = np.concatenate([np.arange(T), 2048 + (np.arange(128) % 8)]).astype(np.float32)
    fr = np.exp(-math.log(10000.0) * np.arange(64, dtype=np.float32) / 64).astype(np.float32)
    ang = pos[:, None] * fr[None, :]
    c64, s64 = np.cos(ang), np.sin(ang)
    fa = np.exp(-math.log(500000.0) * np.arange(8, dtype=np.float32) / 8).astype(np.float32)
    anga = pos[:, None] * fa[None, :]
    c8, s8 = np.cos(anga), np.sin(anga)
    rope = np.concatenate([c64, c64, -s64, s64, c8, c8, -s8, s8], axis=1).astype(np.float32)
    rope = rope.reshape(NB + 1, 128, 288)
    lg = [math.log1p(-2.0 ** (-5.0 - h)) for h in range(4)]
    c = np.arange(128, dtype=np.float64)
    sq = c // 8
    jj = c % 8
    dtab = np.zeros((2, 128, 4, 128), np.float64)
    misc = np.zeros((2, 128, 64), np.float64)
    cb = np.zeros((2, 128, 128), np.float64)
    for h in range(4):
        diff = c[None, :] - c[:, None]
        dtab[0, :, h, :] = np.where(diff >= 0, np.exp(np.maximum(diff, 0) * lg[h]), 0.0)
        same = (sq[None, :] == sq[:, None]) & (diff >= 0)
        dtab[1, :, h, :] = np.where(same, np.exp(np.maximum(diff, 0) * lg[h]), 0.0)
        misc[0, :, h] = np.exp((c + 1.0) * lg[h])
        misc[0, :, 4 + h] = np.exp((127.0 - c) * lg[h])
        misc[0, :, 8 + h] = np.exp(128.0 * lg[h])
        misc[1, :, h] = np.exp((jj + 1.0) * lg[h])
        misc[1, :, 4 + h] = np.exp((7.0 - jj) * lg[h])
        misc[1, :, 8 + h] = np.exp(8.0 * lg[h])
    misc[:, :, 12] = -0.5
    misc[:, :, 13] = -1.0
    for k in range(KBIS):
        misc[:, :, 16 + k] = 2.0 ** -(k + 1)
    cb[0] = np.where(c[None, :] <= c[:, None], 0.0, NEG)
    cb[1] = np.where((sq[None, :] == sq[:, None]) & (c[None, :] <= c[:, None]), 0.0, NEG)
    rmask = np.zeros((128, 32), np.float32)
    for s in range(16):
        rmask[s * 8:(s + 1) * 8, s] = 1.0
    rmask[:, 16] = np.arange(128) % 8
    sel = np.zeros((128, 4, 128), np.float32)
    for g in range(4):
        for k in range(4):
            for q in range(8):
                sel[32 * k + q, g, 8 * (4 * g + k) + q] = 1.0
    return dict(c_ident=ident, c_bigi=bigi, c_rope=rope,
                c_dtab=dtab.reshape(2, 128, 512).astype(np.float32), c_misc=misc.astype(np.float32),
                c_cbias=cb.astype(np.float32), c_rmask=rmask, c_sel=sel.reshape(128, 512))


_F = lambda a: np.ascontiguousarray(np.asarray(a, dtype=np.float32))


def run(cfg, inputs):
    T, DEPTH = cfg["T"], cfg["DEPTH"]
    SAMPLE = cfg.get("SAMPLE", True)
    bld = Builder(cfg)
    nc = bld.build()
    consts = host_consts(T)
    f = _F
    shared = dict(consts)
    shared["w_in"] = f(inputs["w_in"]); shared["w_or"] = f(inputs["w_o_ret"]); shared["w_oa"] = f(inputs["w_o_att"])
    shared["w_out"] = f(inputs["w_out"]); shared["w_pg"] = f(inputs["w_ple_gate"])
    shared["w_pp"] = f(inputs["w_ple_proj"])
    shared["ng"] = f(np.asarray(inputs["norm_gain"]).reshape(DEPTH, 8, 128).transpose(0, 2, 1))
    shared["qg"] = f(inputs["q_norm_gain"]); shared["kg"] = f(inputs["k_norm_gain"])
    if SAMPLE:
        npool = cfg.get("NPOOL", 2560)
        for l_ in range(DEPTH):
            shared["ck%d" % l_] = f(inputs["cache_k"][l_]).reshape(npool * 8, 2048)
            shared["cv%d" % l_] = f(inputs["cache_v"][l_]).reshape(npool * 8, 2048)
            shared["ci%d" % l_] = f(inputs["cache_idx_k"][l_]).reshape(npool * 8, 1024)
        ptab = np.asarray(inputs["page_table"]).astype(np.int32)
    in_maps = []
    for c in range(NCORES):
        b = c // 2
        m = dict(shared)
        m["xp"] = f(inputs["x_prompt"][b])
        m["pp"] = f(inputs["p_prompt"][:, b])
        if SAMPLE:
            sl = slice(c * 16, (c + 1) * 16)
            m["xs"] = f(inputs["x_sample"][sl]).reshape(128, D)
            m["pps"] = f(inputs["p_sample"][:, sl]).reshape(DEPTH, 128, PLE)
            m["st_in"] = f(inputs["state_ret"][:, sl])
            m["ptrep"] = np.ascontiguousarray(np.repeat(ptab[sl], 8, axis=1).T.astype(np.int32))
        in_maps.append(m)
    res = run_bass_kernel_spmd(nc, in_maps, core_ids=list(range(NCORES)))
    return bld, res


def kernel(x_prompt, x_sample, cache_k, cache_v, cache_idx_k, state_ret, page_table,
           p_prompt, p_sample, norm_gain, w_in, q_norm_gain, k_norm_gain, w_o_ret,
           w_o_att, w_out, w_ple_gate, w_ple_proj):
    inputs = dict(x_prompt=x_prompt, x_sample=x_sample, cache_k=cache_k, cache_v=cache_v,
                  cache_idx_k=cache_idx_k, state_ret=state_ret, page_table=page_table, p_prompt=p_prompt,
                  p_sample=p_sample, norm_gain=norm_gain, w_in=w_in, q_norm_gain=q_norm_gain,
                  k_norm_gain=k_norm_gain, w_o_ret=w_o_ret, w_o_att=w_o_att, w_out=w_out,
                  w_ple_gate=w_ple_gate, w_ple_proj=w_ple_proj)
    B, T = np.asarray(x_prompt).shape[:2]
    DEPTH = np.asarray(w_in).shape[0]
    npool = np.asarray(cache_k).shape[1]
    cfg = dict(T=T, DEPTH=DEPTH, SAMPLE=True, NPOOL=npool)
    bld, res = run(cfg, inputs)
    return assemble(cfg, res)


def assemble(cfg, res):
    T, DEPTH = cfg["T"], cfg["DEPTH"]
    R = res.results
    yp = np.stack([R[2 * b]["yp"] for b in range(4)]).astype(np.float32)
    kvi = np.stack([R[2 * b]["kvi"] for b in range(4)], axis=1)
    k_p = np.ascontiguousarray(kvi[..., 0:128]).reshape(DEPTH, 4, T, 2, 64)
    v_p = np.ascontiguousarray(kvi[..., 128:256]).reshape(DEPTH, 4, T, 2, 64)
    ik_p = np.ascontiguousarray(kvi[..., 256:320])
    stp = np.stack([R[2 * b]["stp"] for b in range(4)], axis=1)
    st_p = np.ascontiguousarray(stp.reshape(DEPTH, 4, 128, 4, 128).transpose(0, 1, 3, 2, 4))
    ys = np.concatenate([R[c]["ys"].reshape(16, 8, D) for c in range(NCORES)], axis=0)
    kvis = np.concatenate([R[c]["kvis"].reshape(DEPTH, 16, 8, 320) for c in range(NCORES)], axis=1)
    k_s = np.ascontiguousarray(kvis[..., 0:128]).reshape(DEPTH, 128, 8, 2, 64)
    v_s = np.ascontiguousarray(kvis[..., 128:256]).reshape(DEPTH, 128, 8, 2, 64)
    ik_s = np.ascontiguousarray(kvis[..., 256:320])
    st_s = np.concatenate([R[c]["sts"] for c in range(NCORES)], axis=1)
    f = lambda a: np.ascontiguousarray(a, dtype=np.float32)
    return (f(yp), f(ys), f(k_p), f(v_p), f(ik_p), f(st_p), f(k_s), f(v_s), f(ik_s), f(st_s))
```

```python
import math
import threading
import numpy as np
import ml_dtypes
import concourse.bass as bass
import concourse.mybir as mybir
from concourse.bass_utils import run_bass_kernel_spmd

F32 = mybir.dt.float32
BF = mybir.dt.bfloat16
I32 = mybir.dt.int32
AF = mybir.ActivationFunctionType
ALU = mybir.AluOpType
AX = mybir.AxisListType

D = 1024
INW = 5700
NCORES = 8
PLE = 256
BIG = 30000.0
NEG = -1.0e30
KBIS = 12
EPS_N = 1e-6
EPS_G = 1e-5
O_RQ, O_RK, O_RV, O_RZ, O_AQ, O_AK, O_AV, O_AZ, O_IQ, O_IK, O_IW, O_GR, O_GA = (
    0, 512, 1024, 1536, 2048, 2560, 2688, 2816, 3328, 3584, 3648, 3652, 4676)
NA = 3652
GDEC = [1.0 - 2.0 ** (-5.0 - h) for h in range(4)]

ENGS = ("pe", "act", "dve", "pool", "sp")


class Buf:
    __slots__ = ("name", "psum", "w", "rd", "sem", "cnt")

    def __init__(self, name, psum=False):
        self.name = name
        self.psum = psum
        self.w = None
        self.rd = []
        self.sem = None
        self.cnt = 0


class Tile:
    def __init__(self, h, bs):
        self.h = h
        self.bs = bs if isinstance(bs, list) else [bs]
        self.fresh = True

    @property
    def b(self):
        return self.bs[0]

    def __getitem__(self, k):
        return self.h[k]


def _bufs(ts):
    out = []
    for t in ts:
        if isinstance(t, Buf):
            out.append(t)
        else:
            out.extend(t.bs)
    return out


class Op:
    __slots__ = ("eng", "fn", "deps", "needed", "dma", "grp", "signal")

    def __init__(self, eng, fn, dma, grp):
        self.eng = eng
        self.fn = fn
        self.deps = []
        self.needed = False
        self.dma = dma
        self.grp = grp
        self.signal = None


_TL = threading.local()


class Baton:
    def __init__(self):
        self.active = False

    def run(self, fns, quotas):
        n = len(fns)
        if n == 1:
            fns[0]()
            return
        self.n = n
        self.turn = [threading.Semaphore(0) for _ in range(n)]
        self.done = [False] * n
        self.quota = [max(1, q) for q in quotas]
        self.cnt = 0
        self.exc = []
        self.active = True

        def wrap(k, f):
            self.turn[k].acquire()
            _TL.k = k
            try:
                f()
            except BaseException as e:
                self.exc.append(e)
            finally:
                self.done[k] = True
                self.cnt = 0
                nxt = self._next(k)
                if nxt is not None:
                    self.turn[nxt].release()

        th = [threading.Thread(target=wrap, args=(k, f)) for k, f in enumerate(fns)]
        for t_ in th:
            t_.start()
        self.turn[0].release()
        for t_ in th:
            t_.join()
        self.active = False
        _TL.k = None
        if self.exc:
            raise self.exc[0]

    def run_pair(self, fa, fb, qa=1, qb=1):
        self.run([fa, fb], [qa, qb])

    def _next(self, k):
        for d in range(1, self.n + 1):
            j = (k + d) % self.n
            if j != k and not self.done[j]:
                return j
        return None

    def tick(self):
        if not self.active:
            return
        k = _TL.k
        self.cnt += 1
        if self.cnt >= self.quota[k]:
            nxt = self._next(k)
            if nxt is not None:
                self.cnt = 0
                self.turn[nxt].release()
                self.turn[k].acquire()


class Sched:
    def __init__(self, nc):
        self.baton = Baton()
        self.nc = nc
        self.ops = {e: [] for e in ENGS}
        self.dmabufs = []
        self.nops = 0
        self.maxops = 10 ** 9

    def _dep(self, op, p):
        if p is None or p is op:
            return
        if p.eng == "pe" and op.eng == "pe" and p.dma is None and op.dma is None:
            return
        if op.grp is not None and p.grp == op.grp:
            return
        if p not in op.deps:
            op.deps.append(p)
        p.needed = True

    def add(self, eng, fn, reads=(), writes=(), dma=None, grp=None):
        if self.nops >= self.maxops:
            return None
        self.baton.tick()
        op = Op(eng, fn, dma, grp)
        if dma is not None and dma.sem is None:
            dma.sem = True
            self.dmabufs.append(dma)
        rb = _bufs(reads)
        wb = _bufs(writes)
        for b in rb:
            self._dep(op, b.w)
            if b.psum:
                for r in b.rd:
                    if r.eng != eng:
                        self._dep(op, r)
        for b in wb:
            self._dep(op, b.w)
            for r in b.rd:
                self._dep(op, r)
        for b in rb:
            b.rd.append(op)
        for b in wb:
            b.w = op
            b.rd = []
        self.ops[eng].append(op)
        self.nops += 1
        return op

    def emit(self):
        nc = self.nc
        engsem = {e: nc.alloc_semaphore("s_" + e) for e in ENGS}
        for b in self.dmabufs:
            b.sem = nc.alloc_semaphore("d_" + b.name)
        cnt = {e: 0 for e in ENGS}
        for e in ENGS:
            for op in self.ops[e]:
                if op.dma is not None:
                    op.dma.cnt += 16
                    op.signal = (op.dma.sem, op.dma.cnt, 16)
                elif op.needed:
                    cnt[e] += 1
                    op.signal = (engsem[e], cnt[e], 1)
        self.maxcnt = dict(cnt)
        names = {"pe": "tensor", "act": "scalar", "dve": "vector", "pool": "gpsimd", "sp": "sync"}
        with nc.Block() as block:
            for e in ENGS:
                ops = self.ops[e]

                def body(eng, ops=ops, e=e):
                    seen = {}
                    for op in ops:
                        for p in op.deps:
                            sem, val, _ = p.signal
                            k = id(sem)
                            if seen.get(k, 0) < val:
                                eng.wait_ge(sem, val)
                                seen[k] = val
                        inst = op.fn(eng)
                        if op.signal is not None:
                            inst.then_inc(op.signal[0], op.signal[2])
                    if e == "sp":
                        for b in self.dmabufs:
                            if b.cnt > 0:
                                eng.wait_ge(b.sem, b.cnt)
                        for e2 in ENGS:
                            if e2 != "sp" and cnt[e2] > 0:
                                eng.wait_ge(engsem[e2], cnt[e2])

                getattr(block, names[e])(body)


def bcast_mid(ap, n):
    s = ap.shape
    return ap.unsqueeze(1).to_broadcast([s[0], n, s[1]])


def bcast_last(ap, n):
    s = ap.shape
    return ap.unsqueeze(2).to_broadcast([s[0], s[1], n])


class Builder:
    def __init__(self, cfg):
        self.cfg = cfg
        self.T = cfg["T"]
        self.NB = self.T // 128
        self.DEPTH = cfg["DEPTH"]
        self.TOPK = min(256, self.T // 4)
        self.SAMPLE = cfg.get("SAMPLE", True)
        self.NPOOL = cfg.get("NPOOL", 2560)
        self.nc = bass.Bass("TRN2", target_bir_lowering=False)
        self.S = Sched(self.nc)
        self.S.maxops = cfg.get("MAXOPS", 10 ** 9)
        self.tiles = {}

    def sb(self, name, shape, dt):
        h = self.nc.alloc_sbuf_tensor(name, list(shape), dt)
        t = Tile(h, Buf(name))
        self.tiles[name] = t
        return t

    def view(self, name, ap, bufs=None):
        t = Tile(ap, bufs if bufs is not None else Buf(name))
        self.tiles[name] = t
        return t

    def ps(self, name, shape, dt):
        h = self.nc.alloc_psum_tensor(name, list(shape), dt)
        return Tile(h, Buf(name, psum=True))

    def dram(self, name, shape, dt, kind):
        return self.nc.dram_tensor(name, list(shape), dt, kind=kind).ap()

    def dma(self, eng, out_ap, in_ap, sb_tile, reads=(), writes=(), grp=None):
        return self.S.add(eng, lambda e: e.dma_start(out=out_ap, in_=in_ap),
                          reads=reads, writes=writes, dma=sb_tile.b, grp=grp)

    def gather(self, out_ap, in_ap, idx_ap, sb_tile, reads, writes):
        return self.S.add("pool", lambda e: e.indirect_dma_start(
            out=out_ap, out_offset=None, in_=in_ap,
            in_offset=bass.IndirectOffsetOnAxis(ap=idx_ap, axis=0)),
            reads=reads, writes=writes, dma=sb_tile.b)

    def mm(self, bank, out_ap, lhsT, rhs, reads, start=None, tp=None):
        if start is None:
            start = bank.fresh
        bank.fresh = False
        if tp is None:
            fn = lambda e: e.matmul(out_ap, lhsT, rhs, start=start, stop=True, skip_group_check=True)
        else:
            fn = lambda e: e.matmul(out_ap, lhsT, rhs, start=start, stop=True, skip_group_check=True,
                                    tile_position=tp)
        return self.S.add("pe", fn, reads=reads, writes=[bank])

    def tr(self, bank, out_ap, in_ap, ident_ap, reads):
        bank.fresh = False
        return self.S.add("pe", lambda e: e.transpose(out_ap, in_ap, ident_ap),
                          reads=reads, writes=[bank])

    def act(self, out_ap, in_ap, func, reads, writes, scale=1.0, bias=0.0, accum=None):
        if accum is None:
            fn = lambda e: e.activation(out=out_ap, in_=in_ap, func=func, bias=bias, scale=scale)
        else:
            fn = lambda e: e.activation(out=out_ap, in_=in_ap, func=func, bias=bias, scale=scale,
                                        accum_out=accum)
        return self.S.add("act", fn, reads=reads, writes=writes)

    def ts(self, eng, out_ap, in_ap, s1, op0, reads, writes, s2=None, op1=None, accum=None):
        kw = {}
        if op1 is not None:
            kw["op1"] = op1
        if accum is not None:
            kw["accum_out"] = accum
        return self.S.add(eng, lambda e: e.tensor_scalar(out=out_ap, in0=in_ap, scalar1=s1, scalar2=s2,
                                                         op0=op0, **kw),
                          reads=reads, writes=writes)

    def tt(self, eng, out_ap, a_ap, b_ap, op, reads, writes):
        return self.S.add(eng, lambda e: e.tensor_tensor(out=out_ap, in0=a_ap, in1=b_ap, op=op),
                          reads=reads, writes=writes)

    def stt(self, eng, out_ap, a_ap, scalar, b_ap, op0, op1, reads, writes):
        return self.S.add(eng, lambda e: e.scalar_tensor_tensor(out=out_ap, in0=a_ap, scalar=scalar,
                                                                in1=b_ap, op0=op0, op1=op1),
                          reads=reads, writes=writes)

    def red(self, eng, out_ap, in_ap, op, reads, writes):
        return self.S.add(eng, lambda e: e.tensor_reduce(out=out_ap, in_=in_ap, axis=AX.X, op=op),
                          reads=reads, writes=writes)

    def cp(self, eng, out_ap, in_ap, reads, writes):
        return self.S.add(eng, lambda e: e.tensor_copy(out=out_ap, in_=in_ap), reads=reads, writes=writes)

    def memset(self, eng, ap, val, writes):
        return self.S.add(eng, lambda e: e.memset(ap, val), writes=writes)

    def build(self):
        T, NB, DEPTH = self.T, self.NB, self.DEPTH
        TW = max(T, 4096)
        NROW = self.NPOOL * 8
        dr = self.dram
        self.xp = dr("xp", [T, D], F32, "ExternalInput")
        self.pp = dr("pp", [DEPTH, T, PLE], F32, "ExternalInput")
        self.w_in = dr("w_in", [DEPTH, D, INW], F32, "ExternalInput")
        self.w_or = dr("w_or", [DEPTH, 512, D], F32, "ExternalInput")
        self.w_oa = dr("w_oa", [DEPTH, 512, D], F32, "ExternalInput")
        self.w_out = dr("w_out", [DEPTH, D, D], F32, "ExternalInput")
        self.w_pg = dr("w_pg", [DEPTH, D, D], F32, "ExternalInput")
        self.w_pp = dr("w_pp", [DEPTH, PLE, D], F32, "ExternalInput")
        self.ng = dr("ng", [DEPTH, 128, 8], F32, "ExternalInput")
        self.qg = dr("qg", [DEPTH, 64], F32, "ExternalInput")
        self.kg = dr("kg", [DEPTH, 64], F32, "ExternalInput")
        self.c_ident = dr("c_ident", [128, 128], F32, "ExternalInput")
        self.c_bigi = dr("c_bigi", [128, 512], F32, "ExternalInput")
        self.c_rope = dr("c_rope", [NB + 1, 128, 288], F32, "ExternalInput")
        self.c_dtab = dr("c_dtab", [2, 128, 512], F32, "ExternalInput")
        self.c_misc = dr("c_misc", [2, 128, 64], F32, "ExternalInput")
        self.c_cbias = dr("c_cbias", [2, 128, 128], F32, "ExternalInput")
        self.c_rmask = dr("c_rmask", [128, 32], F32, "ExternalInput")
        self.c_sel = dr("c_sel", [128, 512], F32, "ExternalInput")
        self.yp = dr("yp", [T, D], F32, "ExternalOutput")
        self.kvi = dr("kvi", [DEPTH, T, 320], F32, "ExternalOutput")
        self.stp = dr("stp", [DEPTH, 128, 512], F32, "ExternalOutput")
        self.xbuf = dr("xbuf", [T + 128, D], F32, "Internal")
        self.gbuf = dr("gbuf", [T + 128, 1024], BF, "Internal")
        self.xbuf_b = [Buf("xbuf%d" % i) for i in range(NB + 1)]
        self.gbuf_b = [Buf("gbuf%d" % i) for i in range(NB + 1)]
        if self.SAMPLE:
            self.xs = dr("xs", [128, D], F32, "ExternalInput")
            self.pps = dr("pps", [DEPTH, 128, PLE], F32, "ExternalInput")
            self.st_in = dr("st_in", [DEPTH, 16, 4, 128, 128], F32, "ExternalInput")
            self.ck = [dr("ck%d" % l_, [NROW, 2048], F32, "ExternalInput") for l_ in range(DEPTH)]
            self.cv = [dr("cv%d" % l_, [NROW, 2048], F32, "ExternalInput") for l_ in range(DEPTH)]
            self.ci = [dr("ci%d" % l_, [NROW, 1024], F32, "ExternalInput") for l_ in range(DEPTH)]
            self.ptrep = dr("ptrep", [128, 16], I32, "ExternalInput")
            self.ys = dr("ys", [128, D], F32, "ExternalOutput")
            self.kvis = dr("kvis", [DEPTH, 128, 320], F32, "ExternalOutput")
            self.sts = dr("sts", [DEPTH, 16, 4, 128, 128], F32, "ExternalOutput")

        sb, ps, view = self.sb, self.ps, self.view
        self.WA = [sb("WA%d" % c, [128, NA], BF) for c in range(8)]
        self.Qreg = self.nc.alloc_sbuf_tensor("Qreg", [128, 16384], BF)
        q = self.Qreg
        self.scoresB = view("scoresB", q[:, 0:8192].bitcast(F32))
        self.ikTf = view("ikTf", q[0:64, 8192:12288])
        view("aqT2", q[:, 12288:12800])
        view("silu_az2", q[:, 12800:13312])
        view("G2", q[:, 13312:14336])
        self.WPP = sb("WPP", [128, 2, D], BF)
        self.scores = sb("scores", [128, TW], F32)
        ub = self.scores.h[:, 0:4096].bitcast(BF)
        self.WPG = view("WPG", ub.rearrange("p (c n) -> p c n", c=8), self.scores.bs)
        self.qpad = view("qpad", ub.rearrange("p (h s t) -> p h s t", h=4, s=16), self.scores.bs)
        self.ident = sb("ident", [128, 128], BF)
        self.bigi = sb("bigi", [128, 512], BF)
        self.dtab = sb("dtab", [128, 512], F32)
        self.misc = sb("misc", [128, 64], F32)
        self.cbias = sb("cbias", [128, 128], F32)
        self.gcol = sb("gcol", [128, 8], F32)
        self.qgb = sb("qgb", [128, 64], F32)
        self.kgb = sb("kgb", [128, 64], F32)
        self.rope = [sb("rope%d" % i, [128, 288], F32) for i in range(2)]
        self.ikT = sb("ikT", [128, TW // 2], BF)
        self.akT = sb("akT", [128, TW], BF)
        self.vaug = sb("vaug", [128, max(NB, 32), 2, 65], BF)
        self.S32 = sb("S32", [128, 512], F32)
        self.S16 = sb("S16", [128, 512], BF)
        self.xt = sb("xt", [128, D], F32)
        self.xn = sb("xn", [128, D], BF)
        self.hT = sb("hT", [128, 8, 128], BF)
        self.fA = sb("fA", [128, 520], F32)
        self.fB = sb("fB", [128, 520], F32)
        self.fC = sb("fC", [128, 520], F32)
        self.sm = sb("sm", [128, 64], F32)
        self.junk = sb("junk", [128, 4096], BF)
        jb = self.junk.bs
        self.thg = view("thg", self.junk.h[:, 0:2048], jb)
        self.mrg = view("mrg", self.junk.h[:, 2048:3072], jb)
        self.mT = view("mT", self.junk.h[:, 3072:4096].rearrange("p (c t) -> p c t", c=8), jb)
        for n in ["q_tm", "qd_tm", "qT", "qdT", "k_tm", "kT"]:
            sb(n, [128, 512], BF)
        t = self.tiles
        for n in ["R0", "R1", "PT0", "PT1", "PT2", "PT3", "m1a", "m1b", "aqT1", "silu_az1", "diag1"]:
            sb(n, [128, 512], BF)
        self.fD = sb("fD", [128, 520], F32)
        self.smB = sb("smB", [128, 16], F32)
        self.s1reg = self.nc.alloc_sbuf_tensor("s1reg", [128, 4096], BF)
        s1n = ["aq_bf", "aqT", "silu_az", "iq_bf", "v_tm", "v_dec", "innerD", "silu_rz"]
        for k, n in enumerate(s1n):
            view(n, self.s1reg[:, k * 512:(k + 1) * 512])
        view("GT", self.s1reg[:, 0:1024], t["aq_bf"].bs + t["aqT"].bs)
        self.thp = view("thp", self.s1reg[:, 1024:2048], t["silu_az"].bs + t["iq_bf"].bs)
        self.pbf = view("pbf", self.s1reg[:, 2048:2304], t["v_tm"].bs)
        self.pT = view("pT", self.s1reg[:, 2560:2816].rearrange("p (c t) -> p c t", c=2), t["v_dec"].bs)
        sb("diag0", [128, 512], BF)
        sb("G0", [128, 1024], BF)
        sb("G1", [128, 1024], BF)
        sb("iqT0", [128, 512], BF)
        sb("iqT1", [128, 512], BF)
        t["aqT0"] = t["aqT"]
        t["silu_az0"] = t["silu_az"]
        t["G"] = t["G0"]
        nblk = TW // 128
        self.ikTf_blk = [Tile(self.Qreg[0:64, 8192 + j * 128:8192 + (j + 1) * 128], Buf("ikTf_b%d" % j))
                         for j in range(32)]
        self.ikTf.bs = [x.b for x in self.ikTf_blk]
        qb_all = (self.scoresB.bs + self.ikTf.bs + t["aqT2"].bs + t["silu_az2"].bs + t["G2"].bs
                  + [Buf("Qrest")])
        self.WOR = Tile(q[:, 0:4096].rearrange("p (c n) -> p c n", c=4), qb_all)
        self.WOA = Tile(q[:, 4096:8192].rearrange("p (c n) -> p c n", c=4), qb_all)
        self.WOUT = Tile(q[:, 8192:16384].rearrange("p (c n) -> p c n", c=8), qb_all)
        self.akT_blk = [Tile(self.akT.h[:, j * 128:(j + 1) * 128], Buf("akT_b%d" % j)) for j in range(nblk)]
        self.ikT_blk = [Tile(self.ikT.h[(j % 2) * 64:(j % 2) * 64 + 64, (j // 2) * 128:(j // 2 + 1) * 128],
                             Buf("ikT_b%d" % j)) for j in range(nblk)]
        nvb = max(NB, 32)
        self.vaug_blk = [Tile(self.vaug.h[:, j, :, :], Buf("vaug_b%d" % j)) for j in range(nvb)]
        self.akT.bs = [x.b for x in self.akT_blk]
        self.ikT.bs = [x.b for x in self.ikT_blk]
        self.vaug.bs = [x.b for x in self.vaug_blk]
        self.kvi_t = sb("kvi_t", [128, 320], F32)
        self.akbf = sb("akbf", [128, 128], BF)
        self.ikbf = sb("ikbf", [128, 128], BF)
        self.bis = sb("bis", [128, 8 + KBIS], F32)
        self.bisB = sb("bisB", [128, 8 + KBIS], F32)
        self.scoresP = [self.scores, self.scoresB]
        self.bisP = [self.bis, self.bisB]
        self.pt = sb("pt", [128, PLE], F32)
        if self.SAMPLE:
            self.ones = sb("ones", [128, 128], BF)
            self.rmask = sb("rmask", [128, 32], F32)
            self.sel = sb("sel", [128, 512], BF)
            self.ptr_i = sb("ptr_i", [128, 16], I32)
            self.ptr_f = sb("ptr_f", [128, 16], F32)
            self.pidx = sb("pidx", [128, 16], I32)
            self.Kb = sb("Kb", [128, 2048], BF)
            self.dsel = view("dsel", self.Kb.h[:, :], self.Kb.bs)
            self.Vb = view("Vb", self.s1reg[:, 2048:4096],
                           t["v_tm"].bs + t["v_dec"].bs + t["innerD"].bs + t["silu_rz"].bs)
            self.Ib = sb("Ib", [128, 1024], BF)
            self.aknT = sb("aknT", [128, 128], BF)
            self.iknT = sb("iknT", [128, 128], BF)
            self.vnew = sb("vnew", [128, 2, 65], BF)
        print("sbuf bytes remaining", self.nc.sbuf_bytes_remaining)
        ab = self.akT.bs
        vf = self.vaug.h[:, 0:32, :, :].rearrange("p a b c -> p (a b c)")
        vb = self.vaug.bs
        ib = self.ikT.bs
        sbs = self.S32.bs
        T_ = Tile
        self.S2 = [
            dict(xt=self.xt, xn=self.xn, hT=self.hT, thg=self.thg, mrg=self.mrg, mT=self.mT, GT=t["GT"],
                 thp=self.thp, pbf=self.pbf, pT=self.pT, pt=self.pt, G=t["G0"], fA=self.fA, fB=self.fB,
                 sm=self.sm),
            dict(xt=T_(self.akT.h[:, 0:2048].bitcast(F32), ab[0:16]), thg=T_(self.akT.h[:, 2048:4096], ab[16:32]),
                 xn=T_(vf[:, 0:1024], vb), hT=T_(vf[:, 1024:2048].rearrange("p (c t) -> p c t", c=8), vb),
                 mrg=T_(vf[:, 2048:3072], vb), mT=T_(vf[:, 3072:4096].rearrange("p (c t) -> p c t", c=8), vb),
                 GT=T_(self.ikT.h[:, 0:1024], ib), thp=T_(self.ikT.h[:, 1024:2048], ib),
                 pt=T_(self.S32.h[:, 0:256], sbs), pbf=T_(self.S32.h[:, 256:384].bitcast(BF), sbs),
                 pT=T_(self.S32.h[:, 384:512].bitcast(BF).rearrange("p (c t) -> p c t", c=2), sbs),
                 G=t["G1"], fA=self.fD, fB=self.fC, sm=self.smB),
        ]
        self.PB = [ps("PB%d" % i, [128, 512], F32) for i in range(8)]
        self.PT = [Tile(self.PB[6 + i].h[:, :].bitcast(BF), self.PB[6 + i].bs) for i in range(2)]
        self._pbi = 0
        self._pti = 0
        self._pa = 0
        self._pb = 0
        self._s2mode = False
        self._s1mode = False
        self._yfree = False

        self.load_consts()
        for l in range(DEPTH):
            self.load_layer_small(l)
            self.load_WA(l)
            self.load_tabs(0)
            if l > 0:
                self.memset("pool", self.vaug[:, :, :, 64:65], 1.0, [self.vaug])
            self.sweep1_A(l, 0)
            pipe = self.cfg.get("PIPE", True)
            for i in range(NB + 1):
                fns, qs = [], []
                if i + 1 < NB:
                    fns.append(lambda i=i: self.sweep1_A(l, i + 1)); qs.append(270)
                else:
                    fns.append(lambda: None); qs.append(1)
                if i < NB:
                    nch = (i + 4) // 4
                    fns.append(lambda i=i: self.indexer(l, i, i % 2)); qs.append(13 * nch + 50)
                else:
                    fns.append(lambda: None); qs.append(1)
                if i >= 1:
                    fns.append(lambda i=i: self.attention(l, i - 1, (i - 1) % 2)); qs.append(14 * i + 12)
                else:
                    fns.append(lambda: None); qs.append(1)
                if pipe:
                    nz = 14 * i + 12
                    self._q2 = [max(1, round(270 * 0.8 / (3 * KBIS + 6))), 1, max(1, round(nz * 0.8 / (3 * KBIS + 6)))]
                    self._s1mode = True
                    self._yfree = (i >= NB)
                    self._pb = 0
                    self.S.baton.run(fns, [2, 6, 2])
                    self._s1mode = False
                    self._yfree = False
                else:
                    for f_ in fns:
                        f_()
            self.store_state(l)
            self.load_W2(l)
            if self.SAMPLE:
                self.load_tabs(1)
                self.sweep1_block(l, NB, sample=True)
            self.load_gates(l)
            self.load_WPG(l)
            n2 = NB + (1 if self.SAMPLE else 0)
            self._s2mode = True
            self.sweep2_P(l, 0)
            for i in range(n2):
                if i + 1 < n2 and self.cfg.get("PIPE", True):
                    self.S.baton.run_pair(lambda i=i: self.sweep2_P(l, i + 1), lambda i=i: self.sweep2_Q(l, i), 1, 1)
                else:
                    if i + 1 < n2:
                        self.sweep2_P(l, i + 1)
                    self.sweep2_Q(l, i)
            self._s2mode = False
        self.S.emit()
        return self.nc

    def bank(self):
        k = getattr(_TL, "k", None) if self.S.baton.active else None
        if k is None:
            b = self.PB[self._pbi % 4]
            self._pbi += 1
        elif self._s1mode:
            if k == 0:
                pool = (self.PB[0], self.PB[1], self.PB[2]) if self._yfree else (self.PB[0],)
                b = pool[self._pa % len(pool)]
                self._pa += 1
            elif k == 1:
                b = self.PB[1]
            else:
                b = (self.PB[3], self.PB[7])[self._pb % 2]
                self._pb += 1
        elif k == 0:
            b = self.PB[self._pa % 2]
            self._pa += 1
        else:
            b = self.PB[2 + self._pb % 2]
            self._pb += 1
        b.fresh = True
        return b

    def tbank(self):
        k = getattr(_TL, "k", None) if self.S.baton.active else None
        if k is not None and self._s2mode:
            return self.PT[k]
        if k is not None and self._s1mode:
            return self.PT[0]
        b = self.PT[self._pti % 2]
        self._pti += 1
        return b

    def load_consts(self):
        self.dma("pool", self.ident[:, :], self.c_ident[:, :], self.ident, writes=[self.ident])
        self.dma("pool", self.bigi[:, :], self.c_bigi[:, :], self.bigi, writes=[self.bigi])
        self.memset("pool", self.vaug[:, :, :, :], 1.0, [self.vaug])
        self.memset("pool", self.ikbf[:, :], 0.0, [self.ikbf])
        if self.SAMPLE:
            self.dma("sp", self.rmask[:, :], self.c_rmask[:, :], self.rmask, writes=[self.rmask])
            self.dma("pool", self.sel[:, :], self.c_sel[:, :], self.sel, writes=[self.sel])
            self.dma("sp", self.ptr_i[:, :], self.ptrep[:, :], self.ptr_i, writes=[self.ptr_i])
            self.memset("pool", self.vnew[:, :, :], 1.0, [self.vnew])
            self.memset("pool", self.ones[:, :], 1.0, [self.ones])
            self.cp("pool", self.ptr_f[:, :], self.ptr_i[:, :], [self.ptr_i], [self.ptr_f])
            self.ts("pool", self.ptr_f[:, :], self.ptr_f[:, :], 8.0, ALU.mult, [self.ptr_f], [self.ptr_f])
            self.tt("pool", self.ptr_f[:, :], self.ptr_f[:, :], self.rmask[:, 16:17].to_broadcast([128, 16]),
                    ALU.add, [self.ptr_f, self.rmask], [self.ptr_f])
            self.cp("pool", self.pidx[:, :], self.ptr_f[:, :], [self.ptr_f], [self.pidx])

    def load_tabs(self, k):
        self.dma("sp", self.dtab[:, :], self.c_dtab[k], self.dtab, writes=[self.dtab])
        self.dma("sp", self.misc[:, :], self.c_misc[k], self.misc, writes=[self.misc])
        self.dma("sp", self.cbias[:, :], self.c_cbias[k], self.cbias, writes=[self.cbias])

    def load_layer_small(self, l):
        self.dma("sp", self.gcol[:, :], self.ng[l], self.gcol, writes=[self.gcol])
        self.dma("sp", self.qgb[:, :], self.qg[l].partition_broadcast(128), self.qgb, writes=[self.qgb])
        self.dma("sp", self.kgb[:, :], self.kg[l].partition_broadcast(128), self.kgb, writes=[self.kgb])

    def load_WA(self, l):
        for c in range(8):
            self.dma("pool", self.WA[c][:, :], self.w_in[l, c * 128:(c + 1) * 128, 0:NA], self.WA[c],
                     writes=[self.WA[c]])

    def load_gates(self, l):
        for c in range(8):
            self.dma("pool", self.WA[c][:, 0:2048], self.w_in[l, c * 128:(c + 1) * 128, O_GR:O_GR + 2048],
                     self.WA[c], writes=[self.WA[c]])

    def load_WPG(self, l):
        self.dma("pool", self.WPG[:, :, :], self.w_pg[l].rearrange("(c p) n -> p c n", p=128), self.WPG,
                 writes=[self.WPG])

    def load_W2(self, l):
        for wt, src in ((self.WOR, self.w_or), (self.WOA, self.w_oa), (self.WOUT, self.w_out),
                        (self.WPP, self.w_pp)):
            self.dma("pool", wt[:, :, :], src[l].rearrange("(c p) n -> p c n", p=128), wt, writes=[wt])

    def norm_hT(self, src_ap, src_bufs, ts=None):
        if ts is None:
            xt, xn, hT, sm = self.xt, self.xn, self.hT, self.sm
        else:
            xt, xn, hT, sm = ts["xt"], ts["xn"], ts["hT"], ts["sm"]
        self.dma("sp", xt[:, :], src_ap, xt, reads=src_bufs, writes=[xt])
        self.act(xn[:, :], xt[:, :], AF.Square, [xt], [xn, sm], accum=sm[:, 0:1])
        self.ts("pool", sm[:, 1:2], sm[:, 0:1], 1.0 / D, ALU.mult, [sm], [sm], s2=EPS_N, op1=ALU.add)
        self.tt("pool", sm[:, 2:3], sm[:, 1:2], self.misc[:, 12:13], ALU.pow, [sm, self.misc], [sm])
        self.act(xn[:, :], xt[:, :], AF.Copy, [xt, sm], [xn], scale=sm[:, 2:3])
        tb = self.tbank()
        for c in range(8):
            self.tr(tb, tb[:, c * 128:(c + 1) * 128], xn[:, c * 128:(c + 1) * 128], self.ident[:, :],
                    [xn, self.ident])
        self.tt("dve", hT[:, :, :], tb[:, :].rearrange("p (c t) -> p c t", c=8),
                bcast_last(self.gcol[:, :], 128), ALU.mult, [tb, self.gcol], [hT])

    def proj(self, bank, w_tiles, col0, ncols, hT=None):
        hT = self.hT if hT is None else hT
        for c in range(8):
            self.mm(bank, bank[:, 0:ncols], hT[:, c, :], w_tiles[c][:, col0:col0 + ncols],
                    [hT, w_tiles[c]])

    def rope_big(self, src, dst, tab):
        fB, fC = self.fB, self.fC
        cs = tab[:, 0:128]
        sn = tab[:, 128:256]
        s4 = src[:, 0:512].rearrange("p (h two d) -> p h two d", h=4, two=2)
        c4 = fC[:, 0:512].rearrange("p (h two d) -> p h two d", h=4, two=2)
        self.tt("pool", fB[:, 0:512].rearrange("p (h d) -> p h d", h=4),
                src[:, 0:512].rearrange("p (h d) -> p h d", h=4), bcast_mid(cs, 4), ALU.mult,
                [src, tab], [fB])
        self.tt("pool", c4[:, :, 0, :], s4[:, :, 1, :], bcast_mid(sn[:, 0:64], 4), ALU.mult, [src, tab], [fC])
        self.tt("pool", c4[:, :, 1, :], s4[:, :, 0, :], bcast_mid(sn[:, 64:128], 4), ALU.mult, [src, tab], [fC])
        self.tt("pool", dst[:, 0:512], fB[:, 0:512], fC[:, 0:512], ALU.add, [fB, fC], [dst])

    def headnorm_rope(self, fA, fB, nh, gain, tab):
        sm = self.sm
        w = nh * 64
        a3 = fA[:, 0:w].rearrange("p (h d) -> p h d", h=nh)
        self.red("dve", sm[:, 16:16 + nh], fB[:, 0:w].rearrange("p (h d) -> p h d", h=nh), ALU.add, [fB], [sm])
        self.ts("pool", sm[:, 24:24 + nh], sm[:, 16:16 + nh], 1.0 / 64, ALU.mult, [sm], [sm], s2=EPS_N, op1=ALU.add)
        self.tt("pool", sm[:, 32:32 + nh], sm[:, 24:24 + nh], self.misc[:, 12:13].to_broadcast([128, nh]), ALU.pow,
                [sm, self.misc], [sm])
        self.tt("pool", a3, a3, bcast_last(sm[:, 32:32 + nh], 64), ALU.mult, [fA, sm], [fA])
        self.tt("pool", a3, a3, bcast_mid(gain[:, :], nh), ALU.mult, [fA, gain], [fA])
        self.rope16(a3, nh, tab)

    def rope16(self, a3, nh, tab):
        fA, fC = self.fA, self.fC
        cs = tab[:, 256:272]
        sn = tab[:, 272:288]
        c3 = fC[:, 0:nh * 32].rearrange("p (h d) -> p h d", h=nh)
        self.tt("pool", c3[:, :, 0:16], a3[:, :, 0:16], bcast_mid(cs, nh), ALU.mult, [fA, tab], [fC])
        self.tt("pool", c3[:, :, 16:24], a3[:, :, 8:16], bcast_mid(sn[:, 0:8], nh), ALU.mult, [fA, tab], [fC])
        self.tt("pool", c3[:, :, 24:32], a3[:, :, 0:8], bcast_mid(sn[:, 8:16], nh), ALU.mult, [fA, tab], [fC])
        self.tt("pool", a3[:, :, 0:16], c3[:, :, 0:16], c3[:, :, 16:32], ALU.add, [fC], [fA])

    def sweep1_block(self, l, i, sample=False):
        self.sweep1_A(l, i, sample)
        if not sample:
            self.sweep1_B(l, i)

    def sweep1_B(self, l, i):
        self.indexer(l, i, i % 2)
        self.attention(l, i, i % 2)

    def sweep1_A(self, l, i, sample=False):
        t = self.tiles
        NB = self.NB
        par = i % 2
        par3 = (i % 2) if sample else (i % 3)
        ident = self.ident
        fA, fB, fC, sm, misc = self.fA, self.fB, self.fC, self.sm, self.misc
        tab = self.rope[i % 2]
        self.dma("sp", tab[:, :], self.c_rope[i], tab, writes=[tab])
        if sample:
            src = self.xs[:, :] if l == 0 else self.xbuf[NB * 128:(NB + 1) * 128, :]
        else:
            src = self.xp[i * 128:(i + 1) * 128, :] if l == 0 else self.xbuf[i * 128:(i + 1) * 128, :]
        self.norm_hT(src, [] if l == 0 else [self.xbuf_b[i]])
        WA = self.WA
        q_tm, qd_tm, qT, qdT = t["q_tm"], t["qd_tm"], t["qT"], t["qdT"]
        k_tm, kT, v_tm, v_dec = t["k_tm"], t["kT"], t["v_tm"], t["v_dec"]
        b = self.bank(); self.proj(b, WA, O_RQ, 512)
        self.act(fA[:, 0:512], b[:, :], AF.Copy, [b], [fA])
        self.rope_big(fA, q_tm, tab)
        self.tt("pool", qd_tm[:, :].rearrange("p (h d) -> p h d", h=4),
                q_tm[:, :].rearrange("p (h d) -> p h d", h=4), bcast_last(misc[:, 0:4], 128), ALU.mult,
                [q_tm, misc], [qd_tm])
        tb = self.tbank()
        for h in range(4):
            self.tr(tb, tb[:, h * 128:(h + 1) * 128], q_tm[:, h * 128:(h + 1) * 128], ident[:, :], [q_tm, ident])
        for h in range(4):
            self.tr(tb, tb[:, 512 + h * 128:512 + (h + 1) * 128], qd_tm[:, h * 128:(h + 1) * 128], ident[:, :],
                    [qd_tm, ident])
        self.cp("dve", qT[:, :], tb[:, 0:512], [tb], [qT])
        self.cp("dve", qdT[:, :], tb[:, 512:1024], [tb], [qdT])
        b = self.bank(); self.proj(b, WA, O_RK, 512)
        self.act(fA[:, 0:512], b[:, :], AF.Copy, [b], [fA], scale=128.0 ** -0.5)
        self.rope_big(fA, k_tm, tab)
        tb = self.tbank()
        for h in range(4):
            self.tr(tb, tb[:, h * 128:(h + 1) * 128], k_tm[:, h * 128:(h + 1) * 128], ident[:, :], [k_tm, ident])
        self.cp("dve", kT[:, :], tb[:, 0:512], [tb], [kT])
        b = self.bank(); self.proj(b, WA, O_RV, 512)
        self.act(v_tm[:, :], b[:, :], AF.Copy, [b], [v_tm])
        self.tt("pool", v_dec[:, :].rearrange("p (h d) -> p h d", h=4),
                v_tm[:, :].rearrange("p (h d) -> p h d", h=4), bcast_last(misc[:, 4:8], 128), ALU.mult,
                [v_tm, misc], [v_dec])
        b = self.bank(); self.proj(b, WA, O_RZ, 512)
        self.act(fA[:, 0:512], b[:, :], AF.Tanh, [b], [fA], scale=0.5)
        self.act(fB[:, 0:512], b[:, :], AF.Copy, [b], [fB], scale=0.5)
        self.tt("pool", fA[:, 0:512], fA[:, 0:512], fB[:, 0:512], ALU.mult, [fA, fB], [fA])
        self.tt("pool", t["silu_rz"][:, :], fA[:, 0:512], fB[:, 0:512], ALU.add, [fA, fB], [t["silu_rz"]])
        self.retention(l, i, sample, par3)
        b = self.bank(); self.proj(b, WA, O_AQ, 512)
        self.act(fA[:, 0:512], b[:, :], AF.Copy, [b], [fA])
        self.act(fB[:, 0:512], b[:, :], AF.Square, [b], [fB])
        self.headnorm_rope(fA, fB, 8, self.qgb, tab)
        aq_bf = t["aq_bf"]
        self.cp("pool", aq_bf[:, :].rearrange("p (h two d) -> p two h d", h=4, two=2),
                fA[:, 0:512].rearrange("p (two h d) -> p two h d", two=2, h=4), [fA], [aq_bf])
        tb = self.tbank()
        for h in range(4):
            self.tr(tb, tb[:, h * 128:(h + 1) * 128], aq_bf[:, h * 128:(h + 1) * 128], ident[:, :], [aq_bf, ident])
        self.cp("dve", t["aqT%d" % par3][:, :], tb[:, 0:512], [tb], [t["aqT%d" % par3]])
        kvi_t = self.kvi_t
        b = self.bank(); self.proj(b, WA, O_AK, 256)
        self.act(fA[:, 0:128], b[:, 0:128], AF.Copy, [b], [fA])
        self.act(fB[:, 0:128], b[:, 0:128], AF.Square, [b], [fB])
        self.act(kvi_t[:, 128:256], b[:, 128:256], AF.Copy, [b], [kvi_t])
        self.headnorm_rope(fA, fB, 2, self.kgb, tab)
        self.cp("pool", kvi_t[:, 0:128], fA[:, 0:128], [fA], [kvi_t])
        self.cp("pool", self.akbf[:, :], fA[:, 0:128], [fA], [self.akbf])
        vdst = self.vnew[:, :, 0:64] if sample else self.vaug_blk[i][:, :, 0:64]
        vt = self.vnew if sample else self.vaug_blk[i]
        self.cp("pool", vdst, kvi_t[:, 128:256].rearrange("p (g d) -> p g d", g=2), [kvi_t], [vt])
        tb = self.tbank()
        self.tr(tb, tb[:, 0:128], self.akbf[:, :], ident[:, :], [self.akbf, ident])
        if sample:
            self.cp("dve", self.aknT[:, :], tb[:, 0:128], [tb], [self.aknT])
        else:
            self.cp("dve", self.akT_blk[i][:, :], tb[:, 0:128], [tb], [self.akT_blk[i]])
        b = self.bank(); self.proj(b, WA, O_AZ, 512)
        self.act(fA[:, 0:512], b[:, :], AF.Tanh, [b], [fA], scale=0.5)
        self.act(fB[:, 0:512], b[:, :], AF.Copy, [b], [fB], scale=0.5)
        saz = t["silu_az%d" % par3]
        self.tt("pool", fA[:, 0:512], fA[:, 0:512], fB[:, 0:512], ALU.mult, [fA, fB], [fA])
        self.tt("pool", saz[:, :], fA[:, 0:512], fB[:, 0:512], ALU.add, [fA, fB], [saz])
        b = self.bank(); self.proj(b, WA, O_IQ, 324)
        self.act(fA[:, 0:324], b[:, 0:324], AF.Copy, [b], [fA])
        self.rope16(fA[:, 0:320].rearrange("p (h d) -> p h d", h=5), 5, tab)
        iq_bf = t["iq_bf"]
        i4 = iq_bf[:, :].rearrange("p (h two d) -> p h two d", h=4, two=2)
        f4 = fA[:, 0:256].rearrange("p (h d) -> p h d", h=4)
        self.cp("pool", i4[:, :, 0, :], f4, [fA], [iq_bf])
        self.cp("pool", i4[:, :, 1, :], f4, [fA], [iq_bf])
        self.cp("pool", kvi_t[:, 256:320], fA[:, 256:320], [fA], [kvi_t])
        half = 0
        self.cp("pool", self.ikbf[:, half * 64:(half + 1) * 64], fA[:, 256:320], [fA], [self.ikbf])
        diag = t["diag%d" % par]
        iqT = t["iqT%d" % par]
        self.ts("pool", sm[:, 8:12], fA[:, 320:324], 0.5, ALU.mult, [fA], [sm])
        self.tt("pool", diag[:, :].rearrange("p (h t) -> p h t", h=4), bcast_mid(ident[:, :], 4),
                bcast_last(sm[:, 8:12], 128), ALU.mult, [ident, sm], [diag])
        tb = self.tbank()
        for h in range(4):
            self.tr(tb, tb[:, h * 128:(h + 1) * 128], iq_bf[:, h * 128:(h + 1) * 128], ident[:, :], [iq_bf, ident])
        self.tr(tb, tb[:, 512:640], self.ikbf[:, :], ident[:, :], [self.ikbf, ident])
        self.cp("dve", iqT[:, :], tb[:, 0:512], [tb], [iqT])
        hs_ = slice(half * 64, (half + 1) * 64)
        if sample:
            self.cp("dve", self.iknT[0:64, :], tb[0:64, 512:640], [tb], [self.iknT])
            self.dma("sp", self.kvis[l], kvi_t[:, :], kvi_t, reads=[kvi_t])
            self.sample_attention(l, par, par3)
        else:
            self.cp("dve", self.ikTf_blk[i][:, :], tb[0:64, 512:640], [tb], [self.ikTf_blk[i]])
            self.dma("sp", self.kvi[l, i * 128:(i + 1) * 128, :], kvi_t[:, :], kvi_t, reads=[kvi_t])

    def retention(self, l, i, sample, par):
        t = self.tiles
        fA, fB, fC, sm, misc, ident = self.fA, self.fB, self.fC, self.sm, self.misc, self.ident
        qT, qdT, kT, k_tm, v_tm, v_dec = t["qT"], t["qdT"], t["kT"], t["k_tm"], t["v_tm"], t["v_dec"]
        innerD = t["innerD"]
        S32, S16 = self.S32, self.S16
        if i == 0 and not sample:
            self.memset("pool", S32[:, :], 0.0, [S32])
            self.memset("pool", S16[:, :], 0.0, [S16])
        b = self.bank()
        for h in range(4):
            hs = slice(h * 128, (h + 1) * 128)
            self.mm(b, b[:, hs], kT[:, hs], qT[:, hs], [kT, qT])
        self.tt("dve", innerD[:, :], b[:, :], self.dtab[:, :], ALU.mult, [b, self.dtab], [innerD])
        if sample:
            b = self.PB[5]
            b.fresh = True
        else:
            b = self.bank()
        if not sample:
            for h in range(4):
                hs = slice(h * 128, (h + 1) * 128)
                self.mm(b, b[:, hs], innerD[:, hs], v_tm[:, hs], [innerD, v_tm])
                self.mm(b, b[:, hs], qdT[:, hs], S16[:, hs], [qdT, S16])
        else:
            for h in range(4):
                hs = slice(h * 128, (h + 1) * 128)
                self.mm(b, b[:, hs], innerD[:, hs], v_tm[:, hs], [innerD, v_tm])
            qpad = self.qpad
            self.memset("pool", qpad[:, :, :, :], 0.0, [qpad])
            q4 = qdT[:, :].rearrange("p (h s q) -> p h s q", h=4, s=16)
            for s in range(16):
                self.cp("pool", qpad[:, :, s, s * 8:(s + 1) * 8], q4[:, :, s, :], [qdT], [qpad])
            for s in range(16):
                sl = slice((s % 2) * 512, (s % 2) * 512 + 512)
                self.dma("sp", self.xt[:, sl].rearrange("p (h e) -> p h e", h=4),
                         self.st_in[l, s].rearrange("h d e -> d h e"), self.xt, writes=[self.xt])
                self.cp("pool", S16[:, :], self.xt[:, sl], [self.xt], [S16])
                for h in range(4):
                    hs = slice(h * 128, (h + 1) * 128)
                    self.mm(b, b[:, hs], qpad[:, h, s, :], S16[:, hs], [qpad, S16])
                kp = t["kT"]
                self.tt("pool", kp[:, :], k_tm[:, :], self.rmask[:, s:s + 1].to_broadcast([128, 512]), ALU.mult,
                        [k_tm, self.rmask], [kp])
                kb = self.bank()
                for h in range(4):
                    hs = slice(h * 128, (h + 1) * 128)
                    self.mm(kb, kb[:, hs], kp[:, hs], v_dec[:, hs], [kp, v_dec])
                self.act(fB[:, 0:512], kb[:, :], AF.Copy, [kb], [fB])
                x4 = self.xt[:, sl].rearrange("p (h d) -> p h d", h=4)
                self.tt("pool", x4, x4, bcast_last(misc[:, 8:12], 128), ALU.mult, [self.xt, misc], [self.xt])
                self.tt("pool", self.xt[:, sl], self.xt[:, sl], fB[:, 0:512], ALU.add, [self.xt, fB], [self.xt])
                self.dma("sp", self.sts[l, s].rearrange("h d e -> d h e"),
                         self.xt[:, sl].rearrange("p (h e) -> p h e", h=4), self.xt, reads=[self.xt])
        self.act(fA[:, 0:512], b[:, :], AF.Copy, [b], [fA])
        self.act(fB[:, 0:512], b[:, :], AF.Square, [b], [fB])
        a3 = fA[:, 0:512].rearrange("p (h d) -> p h d", h=4)
        self.red("dve", sm[:, 40:44], a3, ALU.add, [fA], [sm])
        self.red("dve", sm[:, 44:48], fB[:, 0:512].rearrange("p (h d) -> p h d", h=4), ALU.add, [fB], [sm])
        self.ts("pool", sm[:, 40:44], sm[:, 40:44], 1.0 / 128, ALU.mult, [sm], [sm])
        self.tt("pool", sm[:, 48:52], sm[:, 40:44], sm[:, 40:44], ALU.mult, [sm], [sm])
        self.stt("dve", sm[:, 52:56], sm[:, 44:48], 1.0 / 128, sm[:, 48:52], ALU.mult, ALU.subtract,
                 [sm], [sm])
        self.ts("pool", sm[:, 52:56], sm[:, 52:56], EPS_G, ALU.add, [sm], [sm])
        self.tt("pool", sm[:, 52:56], sm[:, 52:56], misc[:, 12:13].to_broadcast([128, 4]), ALU.pow, [sm, misc], [sm])
        self.tt("pool", a3, a3, bcast_last(sm[:, 40:44], 128), ALU.subtract, [fA, sm], [fA])
        self.tt("pool", a3, a3, bcast_last(sm[:, 52:56], 128), ALU.mult, [fA, sm], [fA])
        G = t["G%d" % par]
        self.tt("pool", G[:, 0:512], fA[:, 0:512], t["silu_rz"][:, :], ALU.mult, [fA, t["silu_rz"]], [G])
        if sample:
            return
        b = self.bank()
        for h in range(4):
            hs = slice(h * 128, (h + 1) * 128)
            self.mm(b, b[:, hs], k_tm[:, hs], v_dec[:, hs], [k_tm, v_dec])
        self.act(fB[:, 0:512], b[:, :], AF.Copy, [b], [fB])
        s4 = S32[:, :].rearrange("p (h d) -> p h d", h=4)
        self.tt("pool", s4, s4, bcast_last(misc[:, 8:12], 128), ALU.mult, [S32, misc], [S32])
        self.tt("pool", S32[:, :], S32[:, :], fB[:, 0:512], ALU.add, [S32, fB], [S32])
        self.cp("pool", S16[:, :], S32[:, :], [S32], [S16])

    def store_state(self, l):
        self.dma("sp", self.stp[l], self.S32[:, :], self.S32, reads=[self.S32])

    def topk(self, nlo, n, TOPK, scores=None, bis=None):
        scores = self.scores if scores is None else scores
        bis = self.bis if bis is None else bis
        misc, junk = self.misc, self.junk
        thr = bis[:, 0:1]
        lo, hi, Rg, mid, cnt, sh = (bis[:, 1:2], bis[:, 2:3], bis[:, 3:4], bis[:, 4:5], bis[:, 5:6], bis[:, 6:7])
        H = bis[:, 8:8 + KBIS]
        self.red("dve", lo, scores[:, 0:nlo], ALU.min, [scores], [bis])
        self.red("dve", hi, scores[:, 0:n], ALU.max, [scores], [bis])
        self.tt("dve", Rg, hi, lo, ALU.subtract, [bis], [bis])
        self.ts("dve", H, misc[:, 16:16 + KBIS], Rg, ALU.mult, [misc, bis], [bis])
        self.stt("dve", mid, Rg, 0.5, lo, ALU.mult, ALU.add, [bis], [bis])
        for k in range(KBIS):
            self.ts("dve", junk[:, 0:n], scores[:, 0:n], mid, ALU.is_ge, [scores, bis], [junk, bis],
                    op1=ALU.add, accum=cnt)
            self.ts("dve", sh, cnt, TOPK - 0.5, ALU.is_ge, [bis], [bis], s2=0.5, op1=ALU.subtract)
            self.stt("dve", mid, sh, H[:, k:k + 1], mid, ALU.mult, ALU.add, [bis], [bis])
        self.stt("dve", thr, Rg, -(2.0 ** -(KBIS + 1)), mid, ALU.mult, ALU.add, [bis], [bis])

    def indexer(self, l, i, par, part=None):
        if part in (None, 0):
            self.indexer_scores(l, i, par)
        if part in (None, 1):
            if i * 128 < self.TOPK:
                self.memset("pool", self.bisP[par][:, 0:1], -1.0e29, [self.bisP[par]])
            else:
                bt = self.S.baton
                if bt.active and self._s1mode:
                    bt.quota[:] = self._q2
                    self._yfree = True
                self.topk(i * 128, (i + 1) * 128, self.TOPK, self.scoresP[par], self.bisP[par])

    def indexer_scores(self, l, i, par):
        t = self.tiles
        n = (i + 1) * 128
        scores = self.scoresP[par]
        diag, iqT = t["diag%d" % par], t["iqT%d" % par]
        R = [t["R0"], t["R1"]]
        nch = (n + 511) // 512
        threaded = self.S.baton.active and self._s1mode
        ri = 0
        for c in range(nch):
            c0 = c * 512
            w = min(512, n - c0)
            bs = self.PB[2] if threaded else self.PB[4]
            bs.fresh = True
            kt = Tile(self.Qreg[0:64, 8192 + c0:8192 + c0 + w],
                      [self.ikTf_blk[j].b for j in range(c * 4, c * 4 + w // 128)])
            for h in range(4):
                b = self.bank()
                self.mm(b, b[:, 0:w], iqT[0:64, h * 128:(h + 1) * 128], kt[:, :], [iqT, kt])
                r = R[ri % 2]; ri += 1
                self.act(r[:, 0:w], b[:, 0:w], AF.Relu, [b], [r], scale=0.125)
                self.mm(bs, bs[:, 0:w], diag[:, h * 128:(h + 1) * 128], r[:, 0:w], [diag, r])
            self.act(scores[:, c0:c0 + w], bs[:, 0:w], AF.Copy, [bs], [scores])
        self.tt("pool", scores[:, i * 128:n], scores[:, i * 128:n], self.cbias[:, :], ALU.add,
                [scores, self.cbias], [scores])

    def attention(self, l, i, par):
        t = self.tiles
        n = (i + 1) * 128
        scores, bis = self.scoresP[par], self.bisP[par]
        par3 = i % 3
        aqT = t["aqT%d" % par3]
        m1 = [t["m1a"], t["m1b"]]
        PTs = [t["PT0"], t["PT1"]]
        thr = bis[:, 0:1]
        oacc = [self.PB[4], self.PB[5]]
        oacc[0].fresh = True
        oacc[1].fresh = True
        pi = 0
        nch = (n + 511) // 512
        for c in range(nch):
            c0 = c * 512
            w = min(512, n - c0)
            mk = m1[c % 2]
            self.ts("dve", mk[:, 0:w], scores[:, c0:c0 + w], thr, ALU.is_ge, [scores, bis], [mk],
                    s2=1.0, op1=ALU.subtract)
            for jj in range(w // 128):
                j = c * 4 + jj
                bb = [self.bank(), self.bank()]
                for g in range(2):
                    ps_ = slice(g * 64, (g + 1) * 64)
                    self.mm(bb[g], bb[g][:, :], self.akT_blk[j][ps_, :], aqT[ps_, :], [self.akT_blk[j], aqT])
                for g in range(2):
                    self.mm(bb[g], bb[g][:, :], mk[:, jj * 128:(jj + 1) * 128], self.bigi[:, :], [mk, self.bigi])
                for g in range(2):
                    pt = t["PT%d" % (2 * (j % 2) + g)]
                    self.act(pt[:, :], bb[g][:, :], AF.Exp, [bb[g]], [pt], scale=0.125)
                    for h in range(4):
                        self.mm(oacc[g], oacc[g][:, h * 65:(h + 1) * 65], pt[:, h * 128:(h + 1) * 128],
                                self.vaug_blk[j][:, g, :], [pt, self.vaug_blk[j]])
        fD = self.fD
        for g in range(2):
            self.act(fD[:, g * 260:(g + 1) * 260], oacc[g][:, 0:260], AF.Copy, [oacc[g]], [fD])
        self.finish_attention(i, par3, fD)

    def finish_attention(self, i, par, src):
        t = self.tiles
        sm = self.smB
        G = t["G%d" % par]
        saz = t["silu_az%d" % par]
        a3 = src[:, 0:520].rearrange("p (h d) -> p h d", h=8)
        self.tt("pool", sm[:, 0:8].unsqueeze(2), a3[:, :, 64:65],
                self.misc[:, 13:14].unsqueeze(1).to_broadcast([128, 8, 1]), ALU.pow, [src, self.misc], [sm])
        self.tt("pool", a3[:, :, 0:64], a3[:, :, 0:64], bcast_last(sm[:, 0:8], 64), ALU.mult, [src, sm], [src])
        self.tt("pool", G[:, 512:1024].rearrange("p (h d) -> p h d", h=8), a3[:, :, 0:64],
                saz[:, :].rearrange("p (h d) -> p h d", h=8), ALU.mult, [src, saz], [G])
        self.dma("sp", self.gbuf[i * 128:(i + 1) * 128, :], G[:, :], G, reads=[G], writes=[self.gbuf_b[i]])

    @staticmethod
    def col0(r):
        return ((r % 2) * 2 + (r // 2) // 4) * 512 + ((r // 2) % 4) * 128

    def sample_attention(self, l, par, par3):
        t = self.tiles
        NB = self.NB
        ident, bigi, sm = self.ident, self.bigi, self.sm
        scores, bis, fA, fB = self.scores, self.bis, self.fA, self.fB
        iqT, diag, aqT = t["iqT%d" % par], t["diag%d" % par], t["aqT%d" % par3]
        R = [t["R0"], t["R1"]]
        PTs = [t["PT0"], t["PT1"]]
        akT, ikT, vaug = self.akT, self.ikT, self.vaug
        Kb, Vb, Ib, dsel = self.Kb, self.Vb, self.Ib, self.dsel
        wb = self.bank()
        self.mm(wb, wb[:, :], self.ones[:, :], diag[:, :], [diag, self.ones])
        for g in range(4):
            self.tt("dve", dsel[:, g * 512:(g + 1) * 512].rearrange("p (h t) -> p h t", h=4), wb[:, :].rearrange(
                "p (h t) -> p h t", h=4), bcast_mid(self.sel[:, g * 128:(g + 1) * 128], 4), ALU.mult,
                [wb, self.sel], [dsel])
        ri = 0
        for g in range(4):
            for k in range(4):
                s = 4 * g + k
                self.gather(Ib[:, :], self.ci[l][:, :], self.pidx[:, s:s + 1], Ib, [self.pidx], [Ib])
                tb = self.tbank()
                for rr in range(8):
                    self.tr(tb, tb[:, rr * 128:(rr + 1) * 128], Ib[:, rr * 128:(rr + 1) * 128], ident[:, :],
                            [Ib, ident])
                self.cp("dve", akT[:, k * 1024:(k + 1) * 1024], tb[:, :], [tb], [akT])
            for cc in range(4):
                half, cq = cc // 2, cc % 2
                hf = slice(half * 64, half * 64 + 64)
                bs = self.PB[4]
                bs.fresh = True
                for h in range(4):
                    b = self.bank()
                    for k in range(4):
                        s = 4 * g + k
                        self.mm(b, b[32 * k:32 * k + 8, :], iqT[hf, h * 128 + 8 * s:h * 128 + 8 * s + 8],
                                akT[hf, k * 1024 + cq * 512:k * 1024 + cq * 512 + 512], [iqT, akT],
                                start=True, tp=(half * 64, 32 * k))
                    r = R[ri % 2]; ri += 1
                    self.act(r[:, :], b[:, :], AF.Relu, [b], [r], scale=0.125)
                    self.mm(bs, bs[:, :], dsel[:, g * 512 + h * 128:g * 512 + (h + 1) * 128], r[:, :], [dsel, r])
                if g == 0:
                    self.act(scores[:, cc * 512:(cc + 1) * 512], bs[:, :], AF.Copy, [bs], [scores])
                else:
                    self.act(fB[:, 0:512], bs[:, :], AF.Copy, [bs], [fB])
                    self.tt("pool", scores[:, cc * 512:(cc + 1) * 512], scores[:, cc * 512:(cc + 1) * 512],
                            fB[:, 0:512], ALU.add, [scores, fB], [scores])
        bs = self.PB[4]
        bs.fresh = True
        for h in range(4):
            b = self.bank()
            self.mm(b, b[:, 0:128], iqT[0:64, h * 128:(h + 1) * 128], self.iknT[0:64, :], [iqT, self.iknT])
            r = R[ri % 2]; ri += 1
            self.act(r[:, 0:128], b[:, 0:128], AF.Relu, [b], [r], scale=0.125)
            self.mm(bs, bs[:, 0:128], diag[:, h * 128:(h + 1) * 128], r[:, 0:128], [diag, r])
        self.act(scores[:, 2048:2176], bs[:, 0:128], AF.Copy, [bs], [scores])
        self.tt("pool", scores[:, 2048:2176], scores[:, 2048:2176], self.cbias[:, :], ALU.add,
                [scores, self.cbias], [scores])
        self.topk(2048, 2176, 256)
        m1s = self.junk
        self.ts("dve", m1s[:, 0:2176], scores[:, 0:2176], bis[:, 0:1], ALU.is_ge, [scores, bis], [m1s],
                s2=1.0, op1=ALU.subtract)
        onew = [self.PB[4], self.PB[5]]
        pi = 0
        for g in range(2):
            onew[g].fresh = True
            ps_ = slice(g * 64, (g + 1) * 64)
            b = self.bank()
            self.mm(b, b[:, :], self.aknT[ps_, :], aqT[ps_, :], [self.aknT, aqT])
            self.mm(b, b[:, :], m1s[:, 2048:2176], bigi[:, :], [m1s, bigi])
            pt = PTs[pi % 2]; pi += 1
            self.act(pt[:, :], b[:, :], AF.Exp, [b], [pt], scale=0.125)
            for h in range(4):
                self.mm(onew[g], onew[g][:, h * 65:(h + 1) * 65], pt[:, h * 128:(h + 1) * 128],
                        self.vnew[:, g, :], [pt, self.vnew])
        for g in range(2):
            self.act(fA[:, g * 260:(g + 1) * 260], onew[g][:, 0:260], AF.Copy, [onew[g]], [fA])
        oT = [self.PB[4], self.PB[5]]
        oT[0].fresh = True
        oT[1].fresh = True
        aq4 = aqT[:, :].rearrange("p (h t) -> p h t", h=4)
        bg4 = bigi[:, :].rearrange("p (h t) -> p h t", h=4)
        for s in range(16):
            slot = s % 2
            self.gather(Kb[:, :], self.ck[l][:, :], self.pidx[:, s:s + 1], Kb, [self.pidx], [Kb])
            self.gather(Vb[:, :], self.cv[l][:, :], self.pidx[:, s:s + 1], Vb, [self.pidx], [Vb])
            for hh in range(2):
                tb = self.tbank()
                for r8 in range(8):
                    r = hh * 8 + r8
                    self.tr(tb, tb[:, r8 * 128:(r8 + 1) * 128], Kb[:, r * 128:(r + 1) * 128], ident[:, :],
                            [Kb, ident])
                self.cp("dve", akT[:, slot * 2048 + hh * 1024:slot * 2048 + (hh + 1) * 1024], tb[:, :], [tb], [akT])
            self.cp("pool", vaug[:, slot * 16:(slot + 1) * 16, :, 0:64],
                    Vb[:, :].rearrange("p (r g d) -> p r g d", r=16, g=2), [Vb], [vaug])
            for g in range(2):
                ps_ = slice(g * 64, (g + 1) * 64)
                b = self.bank()
                for r in range(16):
                    reg = b[:, r * 32:(r + 1) * 32].rearrange("p (h q) -> p h q", h=4)
                    self.mm(b, reg, akT[ps_, slot * 2048 + r * 128:slot * 2048 + (r + 1) * 128],
                            aq4[ps_, :, 8 * s:8 * s + 8], [akT, aqT])
                    c0 = self.col0(r)
                    self.mm(b, reg, m1s[:, c0:c0 + 128], bg4[:, :, 8 * s:8 * s + 8], [m1s, bigi])
                pt = PTs[pi % 2]; pi += 1
                self.act(pt[:, :], b[:, :], AF.Exp, [b], [pt], scale=0.125)
                oreg = oT[g][0:65, :].rearrange("p (h s q) -> p h s q", h=4, s=16)[:, :, s, :]
                for r in range(16):
                    self.mm(oT[g], oreg, vaug[:, slot * 16 + r, g, :],
                            pt[:, r * 32:(r + 1) * 32].rearrange("p (h q) -> p h q", h=4), [vaug, pt])
        oTb = Kb
        for g in range(2):
            self.act(oTb[0:65, g * 512:(g + 1) * 512], oT[g][0:65, :], AF.Copy, [oT[g]], [oTb])
        tb = self.tbank()
        for g in range(2):
            for h in range(4):
                hh = g * 4 + h
                self.tr(tb, tb[:, hh * 66:hh * 66 + 65], oTb[0:65, g * 512 + h * 128:g * 512 + (h + 1) * 128],
                        ident[0:65, 0:65], [oTb, ident])
        self.cp("dve", fB[:, 0:520].rearrange("p (h d) -> p h d", h=8),
                tb[:, 0:528].rearrange("p (h d) -> p h d", h=8)[:, :, 0:65], [tb], [fB])
        self.tt("pool", fA[:, 0:520], fA[:, 0:520], fB[:, 0:520], ALU.add, [fA, fB], [fA])
        self.finish_attention(NB, par3, fA)

    def sweep2_P(self, l, i):
        NB = self.NB
        sample = (i == NB)
        ident = self.ident
        S = self.S2[i % 2]
        xt, fA, fB = S["xt"], S["fA"], S["fB"]
        G, GT, thg, mrg = S["G"], S["GT"], S["thg"], S["mrg"]
        if sample:
            src = self.xs[:, :] if l == 0 else self.xbuf[NB * 128:(NB + 1) * 128, :]
            psrc = self.pps[l]
        else:
            src = self.xp[i * 128:(i + 1) * 128, :] if l == 0 else self.xbuf[i * 128:(i + 1) * 128, :]
            psrc = self.pp[l, i * 128:(i + 1) * 128, :]
        self.dma("sp", S["pt"][:, :], psrc, S["pt"], writes=[S["pt"]])
        self.dma("sp", G[:, :], self.gbuf[i * 128:(i + 1) * 128, :], G, reads=[self.gbuf_b[i]], writes=[G])
        tb = self.tbank()
        for c in range(8):
            self.tr(tb, tb[:, c * 128:(c + 1) * 128], G[:, c * 128:(c + 1) * 128], ident[:, :], [G, ident])
        self.cp("dve", GT[:, :], tb[:, :], [tb], [GT])
        self.norm_hT(src, [] if l == 0 else [self.xbuf_b[i]], S)
        WA = self.WA
        for q in range(4):
            b = self.bank(); self.proj(b, WA, q * 512, 512, S["hT"])
            self.act(thg[:, q * 512:(q + 1) * 512], b[:, :], AF.Tanh, [b], [thg], scale=0.5)
        for hf in range(2):
            cs = slice(hf * 512, (hf + 1) * 512)
            bu = self.bank()
            for c in range(4):
                self.mm(bu, bu[:, :], GT[:, c * 128:(c + 1) * 128], self.WOR[:, c, cs], [GT, self.WOR])
            self.stt("dve", fA[:, 0:512], thg[:, hf * 512:(hf + 1) * 512], 1.0, bu[:, :], ALU.add, ALU.mult,
                     [thg, bu], [fA])
            bu = self.bank()
            for c in range(4):
                self.mm(bu, bu[:, :], GT[:, 512 + c * 128:512 + (c + 1) * 128], self.WOA[:, c, cs], [GT, self.WOA])
            self.stt("dve", fB[:, 0:512], thg[:, 1024 + hf * 512:1024 + (hf + 1) * 512], 1.0, bu[:, :],
                     ALU.add, ALU.mult, [thg, bu], [fB])
            self.tt("pool", mrg[:, cs], fA[:, 0:512], fB[:, 0:512], ALU.add, [fA, fB], [mrg])

    def sweep2_Q(self, l, i):
        NB = self.NB
        sample = (i == NB)
        last = (l == self.DEPTH - 1)
        ident = self.ident
        S = self.S2[i % 2]
        xt, fA, xn = S["xt"], S["fA"], S["xn"]
        mrg, mT, thp, pbf, pT, pt = S["mrg"], S["mT"], S["thp"], S["pbf"], S["pT"], S["pt"]
        tb = self.tbank()
        for c in range(8):
            self.tr(tb, tb[:, c * 128:(c + 1) * 128], mrg[:, c * 128:(c + 1) * 128], ident[:, :], [mrg, ident])
        self.cp("dve", mT[:, :, :], tb[:, :].rearrange("p (c t) -> p c t", c=8), [tb], [mT])
        for hf in range(2):
            cs = slice(hf * 512, (hf + 1) * 512)
            bu = self.bank()
            for c in range(8):
                self.mm(bu, bu[:, :], mT[:, c, :], self.WOUT[:, c, cs], [mT, self.WOUT])
            self.stt("dve", xt[:, cs], bu[:, :], 0.5, xt[:, cs], ALU.mult, ALU.add, [bu, xt], [xt])
        self.cp("pool", xn[:, :], xt[:, :], [xt], [xn])
        tb = self.tbank()
        for c in range(8):
            self.tr(tb, tb[:, c * 128:(c + 1) * 128], xn[:, c * 128:(c + 1) * 128], ident[:, :], [xn, ident])
        self.cp("dve", mT[:, :, :], tb[:, :].rearrange("p (c t) -> p c t", c=8), [tb], [mT])
        self.cp("pool", pbf[:, :], pt[:, :], [pt], [pbf])
        tb = self.tbank()
        for c in range(2):
            self.tr(tb, tb[:, c * 128:(c + 1) * 128], pbf[:, c * 128:(c + 1) * 128], ident[:, :], [pbf, ident])
        self.cp("dve", pT[:, :, :], tb[:, 0:256].rearrange("p (c t) -> p c t", c=2), [tb], [pT])
        for hf in range(2):
            cs = slice(hf * 512, (hf + 1) * 512)
            bu = self.bank()
            for c in range(8):
                self.mm(bu, bu[:, :], mT[:, c, :], self.WPG[:, c, cs], [mT, self.WPG])
            self.act(thp[:, cs], bu[:, :], AF.Tanh, [bu], [thp], scale=0.5)
            bu = self.bank()
            for c in range(2):
                self.mm(bu, bu[:, :], pT[:, c, :], self.WPP[:, c, cs], [pT, self.WPP])
            self.stt("dve", fA[:, 0:512], thp[:, cs], 1.0, bu[:, :], ALU.add, ALU.mult, [thp, bu], [fA])
            self.stt("dve", xt[:, cs], fA[:, 0:512], 0.5, xt[:, cs], ALU.mult, ALU.add, [fA, xt], [xt])
        if last:
            dst = self.ys[:, :] if sample else self.yp[i * 128:(i + 1) * 128, :]
        else:
            dst = self.xbuf[i * 128:(i + 1) * 128, :]
        self.dma("sp", dst, xt[:, :], xt, reads=[xt], writes=[] if last else [self.xbuf_b[i]])


def host_consts(T):
    NB = T // 128
    ident = np.eye(128, dtype=np.float32)
    bigi = np.tile(ident * BIG, (1, 4)).astype(np.float32)
    pos = np.concatenate([np.arange(T), 2048 + (np.arange(128) % 8)]).astype(np.float32)
    fr = np.exp(-math.log(10000.0) * np.arange(64, dtype=np.float32) / 64).astype(np.float32)
    ang = pos[:, None] * fr[None, :]
    c64, s64 = np.cos(ang), np.sin(ang)
    fa = np.exp(-math.log(500000.0) * np.arange(8, dtype=np.float32) / 8).astype(np.float32)
    anga = pos[:, None] * fa[None, :]
    c8, s8 = np.cos(anga), np.sin(anga)
    rope = np.concatenate([c64, c64, -s64, s64, c8, c8, -s8, s8], axis=1).astype(np.float32)
    rope = rope.reshape(NB + 1, 128, 288)
    lg = [math.log1p(-2.0 ** (-5.0 - h)) for h in range(4)]
    c = np.arange(128, dtype=np.float64)
    sq = c // 8
    jj = c % 8
    dtab = np.zeros((2, 128, 4, 128), np.float64)
    misc = np.zeros((2, 128, 64), np.float64)
    cb = np.zeros((2, 128, 128), np.float64)
    for h in range(4):
        diff = c[None, :] - c[:, None]
        dtab[0, :, h, :] = np.where(diff >= 0, np.exp(np.maximum(diff, 0) * lg[h]), 0.0)
        same = (sq[None, :] == sq[:, None]) & (diff >= 0)
        dtab[1, :, h, :] = np.where(same, np.exp(np.maximum(diff, 0) * lg[h]), 0.0)
        misc[0, :, h] = np.exp((c + 1.0) * lg[h])
        misc[0, :, 4 + h] = np.exp((127.0 - c) * lg[h])
        misc[0, :, 8 + h] = np.exp(128.0 * lg[h])
        misc[1, :, h] = np.exp((jj + 1.0) * lg[h])
        misc[1, :, 4 + h] = np.exp((7.0 - jj) * lg[h])
        misc[1, :, 8 + h] = np.exp(8.0 * lg[h])
    misc[:, :, 12] = -0.5
    misc[:, :, 13] = -1.0
    for k in range(KBIS):
        misc[:, :, 16 + k] = 2.0 ** -(k + 1)
    cb[0] = np.where(c[None, :] <= c[:, None], 0.0, NEG)
    cb[1] = np.where((sq[None, :] == sq[:, None]) & (c[None, :] <= c[:, None]), 0.0, NEG)
    rmask = np.zeros((128, 32), np.float32)
    for s in range(16):
        rmask[s * 8:(s + 1) * 8, s] = 1.0
    rmask[:, 16] = np.arange(128) % 8
    sel = np.zeros((128, 4, 128), np.float32)
    for g in range(4):
        for k in range(4):
            for q in range(8):
                sel[32 * k + q, g, 8 * (4 * g + k) + q] = 1.0
    return dict(c_ident=ident, c_bigi=bigi, c_rope=rope,
                c_dtab=dtab.reshape(2, 128, 512).astype(np.float32), c_misc=misc.astype(np.float32),
                c_cbias=cb.astype(np.float32), c_rmask=rmask, c_sel=sel.reshape(128, 512))


_F = lambda a: np.ascontiguousarray(np.asarray(a, dtype=np.float32))


def run(cfg, inputs):
    T, DEPTH = cfg["T"], cfg["DEPTH"]
    SAMPLE = cfg.get("SAMPLE", True)
    bld = Builder(cfg)
    nc = bld.build()
    consts = host_consts(T)
    f = _F
    shared = dict(consts)
    shared["w_in"] = f(inputs["w_in"]); shared["w_or"] = f(inputs["w_o_ret"]); shared["w_oa"] = f(inputs["w_o_att"])
    shared["w_out"] = f(inputs["w_out"]); shared["w_pg"] = f(inputs["w_ple_gate"])
    shared["w_pp"] = f(inputs["w_ple_proj"])
    shared["ng"] = f(np.asarray(inputs["norm_gain"]).reshape(DEPTH, 8, 128).transpose(0, 2, 1))
    shared["qg"] = f(inputs["q_norm_gain"]); shared["kg"] = f(inputs["k_norm_gain"])
    if SAMPLE:
        npool = cfg.get("NPOOL", 2560)
        for l_ in range(DEPTH):
            shared["ck%d" % l_] = f(inputs["cache_k"][l_]).reshape(npool * 8, 2048)
            shared["cv%d" % l_] = f(inputs["cache_v"][l_]).reshape(npool * 8, 2048)
            shared["ci%d" % l_] = f(inputs["cache_idx_k"][l_]).reshape(npool * 8, 1024)
        ptab = np.asarray(inputs["page_table"]).astype(np.int32)
    in_maps = []
    for c in range(NCORES):
        b = c // 2
        m = dict(shared)
        m["xp"] = f(inputs["x_prompt"][b])
        m["pp"] = f(inputs["p_prompt"][:, b])
        if SAMPLE:
            sl = slice(c * 16, (c + 1) * 16)
            m["xs"] = f(inputs["x_sample"][sl]).reshape(128, D)
            m["pps"] = f(inputs["p_sample"][:, sl]).reshape(DEPTH, 128, PLE)
            m["st_in"] = f(inputs["state_ret"][:, sl])
            m["ptrep"] = np.ascontiguousarray(np.repeat(ptab[sl], 8, axis=1).T.astype(np.int32))
        in_maps.append(m)
    res = run_bass_kernel_spmd(nc, in_maps, core_ids=list(range(NCORES)))
    return bld, res


def kernel(x_prompt, x_sample, cache_k, cache_v, cache_idx_k, state_ret, page_table,
           p_prompt, p_sample, norm_gain, w_in, q_norm_gain, k_norm_gain, w_o_ret,
           w_o_att, w_out, w_ple_gate, w_ple_proj):
    inputs = dict(x_prompt=x_prompt, x_sample=x_sample, cache_k=cache_k, cache_v=cache_v,
                  cache_idx_k=cache_idx_k, state_ret=state_ret, page_table=page_table, p_prompt=p_prompt,
                  p_sample=p_sample, norm_gain=norm_gain, w_in=w_in, q_norm_gain=q_norm_gain,
                  k_norm_gain=k_norm_gain, w_o_ret=w_o_ret, w_o_att=w_o_att, w_out=w_out,
                  w_ple_gate=w_ple_gate, w_ple_proj=w_ple_proj)
    B, T = np.asarray(x_prompt).shape[:2]
    DEPTH = np.asarray(w_in).shape[0]
    npool = np.asarray(cache_k).shape[1]
    cfg = dict(T=T, DEPTH=DEPTH, SAMPLE=True, NPOOL=npool)
    bld, res = run(cfg, inputs)
    return assemble(cfg, res)


def assemble(cfg, res):
    T, DEPTH = cfg["T"], cfg["DEPTH"]
    R = res.results
    yp = np.stack([R[2 * b]["yp"] for b in range(4)]).astype(np.float32)
    kvi = np.stack([R[2 * b]["kvi"] for b in range(4)], axis=1)
    k_p = np.ascontiguousarray(kvi[..., 0:128]).reshape(DEPTH, 4, T, 2, 64)
    v_p = np.ascontiguousarray(kvi[..., 128:256]).reshape(DEPTH, 4, T, 2, 64)
    ik_p = np.ascontiguousarray(kvi[..., 256:320])
    stp = np.stack([R[2 * b]["stp"] for b in range(4)], axis=1)
    st_p = np.ascontiguousarray(stp.reshape(DEPTH, 4, 128, 4, 128).transpose(0, 1, 3, 2, 4))
    ys = np.concatenate([R[c]["ys"].reshape(16, 8, D) for c in range(NCORES)], axis=0)
    kvis = np.concatenate([R[c]["kvis"].reshape(DEPTH, 16, 8, 320) for c in range(NCORES)], axis=1)
    k_s = np.ascontiguousarray(kvis[..., 0:128]).reshape(DEPTH, 128, 8, 2, 64)
    v_s = np.ascontiguousarray(kvis[..., 128:256]).reshape(DEPTH, 128, 8, 2, 64)
    ik_s = np.ascontiguousarray(kvis[..., 256:320])
    st_s = np.concatenate([R[c]["sts"] for c in range(NCORES)], axis=1)
    f = lambda a: np.ascontiguousarray(a, dtype=np.float32)
    return (f(yp), f(ys), f(k_p), f(v_p), f(ik_p), f(st_p), f(k_s), f(v_s), f(ik_s), f(st_s))
```
